# Optimizing a Trainium2 kernel written in Bass

```python
import math
import jax, jax.numpy as jnp
from jax import lax
import numpy as np

D_MODEL = 1024
BATCH = 32
SEQ = 2048
DEPTH = 2
DEC_BATCH = 4
DEC_SEQ = 4096
PAST_LEN = 128

HEAD_DIM = 64
GRID_W = 64
Q_BLOCK = 128
EPS = 1e-6
NEG = -1e30

A_Q_HEADS = 8
A_KV_HEADS = 2
AXIAL_THETA = 10000.0
B_HEADS = 4
LAMBDA_SCALE = 0.1
C_HEADS = 16
C_PATTERNS = ((128, 1), (512, 4), (2048, 16))
ROPE_THETA = 500000.0
ROPE_DIMS = HEAD_DIM // 4

A_Q_W = A_Q_HEADS * HEAD_DIM
A_KV_W = A_KV_HEADS * HEAD_DIM
B_QK_W = B_HEADS * 2 * HEAD_DIM
B_V_W = B_HEADS * 2 * HEAD_DIM
EVEN_MIX_W = A_Q_W + B_V_W
EVEN_SPLITS = (A_Q_W, A_KV_W, A_KV_W, B_QK_W, B_QK_W, B_V_W, EVEN_MIX_W)
EVEN_IN_W = sum(EVEN_SPLITS)
C_W = C_HEADS * HEAD_DIM
ODD_SPLITS = (C_W, C_W, C_W, C_W)
ODD_IN_W = sum(ODD_SPLITS)

kernel_name = "hybrid_gqa_diff_dilated_encoder"


def _split_points(sizes):
    pts, acc = [], 0
    for s in sizes[:-1]:
        acc += s
        pts.append(acc)
    return pts


def rms_norm(x, g):
    x32 = x.astype(jnp.float32)
    y = x32 * lax.rsqrt(jnp.mean(x32 * x32, axis=-1, keepdims=True) + EPS)
    return (y * g.astype(jnp.float32)).astype(x.dtype)


def rotate_half(x):
    x1, x2 = jnp.split(x, 2, axis=-1)
    return jnp.concatenate([-x2, x1], axis=-1)


def rope_angles(pos, n_dims, theta):
    freqs = theta ** (-jnp.arange(0, n_dims, 2, dtype=jnp.float32) / n_dims)
    ang = pos[:, None] * freqs[None, :]
    ang = jnp.concatenate([ang, ang], axis=-1)
    return jnp.cos(ang), jnp.sin(ang)


def apply_rope(x, cos, sin):
    S, n = cos.shape
    shp = (1, S) + (1,) * (x.ndim - 3) + (n,)
    c = cos.reshape(shp).astype(x.dtype)
    s = sin.reshape(shp).astype(x.dtype)
    return x * c + rotate_half(x) * s


def partial_rope(x):
    S = x.shape[1]
    pos = jnp.arange(S, dtype=jnp.float32)
    cos, sin = rope_angles(pos, ROPE_DIMS, ROPE_THETA)
    return jnp.concatenate([apply_rope(x[..., :ROPE_DIMS], cos, sin), x[..., ROPE_DIMS:]], axis=-1)


def axial_rope(x):
    S = x.shape[1]
    rows = S // GRID_W
    row = jnp.broadcast_to(jnp.arange(rows, dtype=jnp.float32)[:, None], (rows, GRID_W)).reshape(-1)
    col = jnp.broadcast_to(jnp.arange(GRID_W, dtype=jnp.float32)[None, :], (rows, GRID_W)).reshape(-1)
    half = HEAD_DIM // 2
    xr = apply_rope(x[..., :half], *rope_angles(row, half, AXIAL_THETA))
    xc = apply_rope(x[..., half:], *rope_angles(col, half, AXIAL_THETA))
    return jnp.concatenate([xr, xc], axis=-1)


def split_q_blocks(q):
    B, S = q.shape[:2]
    return jnp.moveaxis(q.reshape((B, S // Q_BLOCK, Q_BLOCK) + q.shape[2:]), 1, 0)


def merge_q_blocks(o):
    nb, B = o.shape[:2]
    return jnp.moveaxis(o, 0, 1).reshape((B, nb * Q_BLOCK) + o.shape[3:])


def gqa_attention(q, k, v):
    B, S, Hq, D = q.shape
    Hkv = k.shape[2]
    G = Hq // Hkv
    scale = D ** -0.5
    qb = split_q_blocks(q.reshape(B, S, Hkv, G, D))

    def block(qi):
        s = jnp.einsum('bqhgd,bkhd->bhgqk', qi, k, preferred_element_type=jnp.float32) * scale
        p = jax.nn.softmax(s, axis=-1).astype(v.dtype)
        return jnp.einsum('bhgqk,bkhd->bqhgd', p, v)

    return merge_q_blocks(lax.map(block, qb)).reshape(B, S, Hq, D)


def diff_attention(q, k, v, lam):
    D = q.shape[-1]
    scale = D ** -0.5
    qb = split_q_blocks(q)

    def block(qi):
        s = jnp.einsum('bqchd,bkchd->bchqk', qi, k, preferred_element_type=jnp.float32) * scale
        p = jax.nn.softmax(s, axis=-1)
        a = (p[:, 0] - lam * p[:, 1]).astype(v.dtype)
        return jnp.einsum('bhqk,bkhe->bqhe', a, v)

    return merge_q_blocks(lax.map(block, qb))


def band_attention(q, k, v, radius):
    B, N, L, H, D = q.shape
    blk = radius
    nb = -(-L // blk)
    Lp = nb * blk
    scale = D ** -0.5
    qp = jnp.pad(q, ((0, 0), (0, 0), (0, Lp - L), (0, 0), (0, 0)))
    kv_pad = ((0, 0), (0, 0), (blk, Lp - L + blk), (0, 0), (0, 0))
    kb = jnp.pad(k, kv_pad).reshape(B, N, nb + 2, blk, H, D)
    vb = jnp.pad(v, kv_pad).reshape(B, N, nb + 2, blk, H, D)
    kw = jnp.concatenate([kb[:, :, :-2], kb[:, :, 1:-1], kb[:, :, 2:]], axis=3)
    vw = jnp.concatenate([vb[:, :, :-2], vb[:, :, 1:-1], vb[:, :, 2:]], axis=3)
    qb = qp.reshape(B, N, nb, blk, H, D)
    qpos = jnp.arange(nb)[:, None] * blk + jnp.arange(blk)[None, :]
    kpos = (jnp.arange(nb)[:, None] - 1) * blk + jnp.arange(3 * blk)[None, :]
    valid = ((jnp.abs(kpos[:, None, :] - qpos[:, :, None]) <= radius)
             & (kpos[:, None, :] >= 0) & (kpos[:, None, :] < L))
    s = jnp.einsum('bniqhd,bnikhd->bnihqk', qb, kw, preferred_element_type=jnp.float32) * scale
    s = jnp.where(valid[None, None, :, None], s, NEG)
    m = jnp.max(s, axis=-1, keepdims=True)
    e = jnp.exp(s - m)
    den = jnp.sum(e, axis=-1, keepdims=True)
    p = (e / den).astype(v.dtype)
    lse = (jnp.log(den) + m)[..., 0]
    o = jnp.einsum('bnihqk,bnikhd->bniqhd', p, vw).reshape(B, N, Lp, H, D)[:, :, :L]
    lse = jnp.swapaxes(lse, 3, 4).reshape(B, N, Lp, H)[:, :, :L]
    return o, lse


def dilated_branch(q, k, v, window, dilation):
    B, S, H, D = q.shape
    L = S // dilation
    radius = window // (2 * dilation)

    def strided(t):
        return jnp.swapaxes(t.reshape(B, L, dilation, H, D), 1, 2)

    o, lse = band_attention(strided(q), strided(k), strided(v), radius)
    o = jnp.swapaxes(o, 1, 2).reshape(B, S, H, D)
    lse = jnp.swapaxes(lse, 1, 2).reshape(B, S, H)
    return o, lse


def dilated_attention(q, k, v):
    outs, lses = [], []
    for window, dilation in C_PATTERNS:
        o, lse = dilated_branch(q, k, v, window, dilation)
        outs.append(o)
        lses.append(lse)
    w = jax.nn.softmax(jnp.stack(lses, axis=0), axis=0)
    o = jnp.sum(w[..., None] * jnp.stack(outs, axis=0).astype(jnp.float32), axis=0)
    return o.astype(q.dtype)


def even_layer(x, norm_g, w_in, w_out, a_q_norm, a_k_norm, b_q_norm, b_k_norm,
               lambda_q1, lambda_k1, lambda_q2, lambda_k2, b_subln, layer_idx):
    B, S, _ = x.shape
    h = rms_norm(x, norm_g)
    z = h @ w_in
    qa, ka, va, qb, kb, vb, gate = jnp.split(z, _split_points(EVEN_SPLITS), axis=-1)
    qa = axial_rope(rms_norm(qa.reshape(B, S, A_Q_HEADS, HEAD_DIM), a_q_norm))
    ka = axial_rope(rms_norm(ka.reshape(B, S, A_KV_HEADS, HEAD_DIM), a_k_norm))
    va = va.reshape(B, S, A_KV_HEADS, HEAD_DIM)
    oa = gqa_attention(qa, ka, va).reshape(B, S, A_Q_W)
    qb = jnp.swapaxes(qb.reshape(B, S, B_HEADS, 2, HEAD_DIM), 2, 3)
    kb = jnp.swapaxes(kb.reshape(B, S, B_HEADS, 2, HEAD_DIM), 2, 3)
    qb = partial_rope(rms_norm(qb, b_q_norm))
    kb = partial_rope(rms_norm(kb, b_k_norm))
    vb = vb.reshape(B, S, B_HEADS, 2 * HEAD_DIM)
    lambda_init = 0.8 - 0.6 * math.exp(-0.3 * layer_idx)
    f32 = jnp.float32
    lam = (jnp.exp(jnp.sum(lambda_q1.astype(f32) * lambda_k1.astype(f32)))
           - jnp.exp(jnp.sum(lambda_q2.astype(f32) * lambda_k2.astype(f32))) + lambda_init)
    ob = diff_attention(qb, kb, vb, lam)
    ob = (rms_norm(ob, b_subln) * (1.0 - lambda_init)).reshape(B, S, B_V_W)
    mix = jnp.concatenate([oa, ob], axis=-1) * jax.nn.silu(gate)
    return x + mix @ w_out


def odd_layer(x, norm_g, w_in, w_out, c_q_norm, c_k_norm):
    B, S, _ = x.shape
    h = rms_norm(x, norm_g)
    z = h @ w_in
    q, k, v, gate = jnp.split(z, _split_points(ODD_SPLITS), axis=-1)
    q = partial_rope(rms_norm(q.reshape(B, S, C_HEADS, HEAD_DIM), c_q_norm))
    k = partial_rope(rms_norm(k.reshape(B, S, C_HEADS, HEAD_DIM), c_k_norm))
    v = v.reshape(B, S, C_HEADS, HEAD_DIM)
    o = dilated_attention(q, k, v).reshape(B, S, C_W)
    return x + (o * jax.nn.silu(gate)) @ w_out


def trunk(x, even_params, odd_params):
    for l in range(DEPTH):
        if l % 2 == 0:
            x = even_layer(x, *even_params, layer_idx=l)
        else:
            x = odd_layer(x, *odd_params)
    return x


def setup_inputs(seed: int = 0) -> dict:
    key = jax.random.key(seed)
    ks = jax.random.split(key, 20)
    f32 = jnp.float32

    def nrm(k, shape, scale):
        return jax.random.normal(k, shape, f32) * scale

    def gain(k, n):
        return jnp.ones((n,), f32) + 0.02 * jax.random.normal(k, (n,), f32)

    return {
        "x_prompt": nrm(ks[0], (BATCH, SEQ, D_MODEL), 1.0),
        "x_sample": nrm(ks[1], (DEC_BATCH, DEC_SEQ, D_MODEL), 1.0),
        "norm0": gain(ks[2], D_MODEL),
        "w_in0": nrm(ks[3], (D_MODEL, EVEN_IN_W), D_MODEL ** -0.5),
        "w_out0": nrm(ks[4], (EVEN_MIX_W, D_MODEL), EVEN_MIX_W ** -0.5),
        "a_q_norm": gain(ks[5], HEAD_DIM),
        "a_k_norm": gain(ks[6], HEAD_DIM),
        "b_q_norm": gain(ks[7], HEAD_DIM),
        "b_k_norm": gain(ks[8], HEAD_DIM),
        "lambda_q1": nrm(ks[9], (HEAD_DIM,), LAMBDA_SCALE),
        "lambda_k1": nrm(ks[10], (HEAD_DIM,), LAMBDA_SCALE),
        "lambda_q2": nrm(ks[11], (HEAD_DIM,), LAMBDA_SCALE),
        "lambda_k2": nrm(ks[12], (HEAD_DIM,), LAMBDA_SCALE),
        "b_subln": gain(ks[13], 2 * HEAD_DIM),
        "norm1": gain(ks[14], D_MODEL),
        "w_in1": nrm(ks[15], (D_MODEL, ODD_IN_W), D_MODEL ** -0.5),
        "w_out1": nrm(ks[16], (C_W, D_MODEL), C_W ** -0.5),
        "c_q_norm": gain(ks[17], HEAD_DIM),
        "c_k_norm": gain(ks[18], HEAD_DIM),
    }


def reference(x_prompt, x_sample, norm0, w_in0, w_out0, a_q_norm, a_k_norm, b_q_norm, b_k_norm,
              lambda_q1, lambda_k1, lambda_q2, lambda_k2, b_subln, norm1, w_in1, w_out1,
              c_q_norm, c_k_norm):
    even_params = (norm0, w_in0, w_out0, a_q_norm, a_k_norm, b_q_norm, b_k_norm,
                   lambda_q1, lambda_k1, lambda_q2, lambda_k2, b_subln)
    odd_params = (norm1, w_in1, w_out1, c_q_norm, c_k_norm)
    y_prompt = trunk(x_prompt, even_params, odd_params)
    y_sample = trunk(x_sample, even_params, odd_params)
    return (y_prompt, y_sample)
```

```python
import math
from contextlib import ExitStack

import numpy as np
import ml_dtypes

import concourse.bass as bass
import concourse.mybir as mybir
from concourse.bass_utils import run_bass_kernel_spmd

F32 = mybir.dt.float32
BF16 = mybir.dt.bfloat16
AF = mybir.ActivationFunctionType
ALU = mybir.AluOpType

D = 1024
EPS = 1e-6
MASK_U0 = 1408
MASK_W = 2944


class Chan:
    def __init__(self, name, step):
        self.name = name
        self.step = step
        self.handle = None
        self.count = 0


class Op:
    __slots__ = ("eng", "fn", "waits", "chan", "pos", "ticket", "need_inc", "is_dma")


class Res:
    __slots__ = ("name", "w", "r", "excl")

    def __init__(self, name, excl=False):
        self.name = name
        self.w = None
        self.r = {}
        self.excl = excl


ENGS = ("sp", "act", "dve", "pool", "pe")


class Prog:
    def __init__(self):
        self.ops = {e: [] for e in ENGS}
        self.echan = {e: Chan(e, 1) for e in ENGS}
        self.dchans = []
        self.synced = {e: {} for e in ENGS}
        self.out_ops = []

    def dchan(self, name):
        c = Chan(name, 16)
        self.dchans.append(c)
        return c

    def emit(self, eng, fn, reads=(), writes=(), dma=None):
        op = Op()
        op.eng = eng
        op.fn = fn
        op.waits = []
        op.need_inc = False
        op.ticket = None
        if dma is not None:
            op.chan = dma
            dma.count += 1
            op.pos = dma.count
            op.is_dma = True
        else:
            op.chan = self.echan[eng]
            op.pos = len(self.ops[eng])
            op.is_dma = False
        deps = {}

        own = self.echan[eng]

        def add(d, war=False):
            if d is None:
                return
            k = d.chan
            if k is own and not d.is_dma:
                if eng == "pe":
                    return
            if k not in deps or deps[k].pos < d.pos:
                deps[k] = d

        for r in reads:
            add(r.w)
            if r.excl:
                for k2, rr in r.r.items():
                    if k2 is not op.chan:
                        add(rr)
        for w in writes:
            add(w.w)
            for rr in w.r.values():
                add(rr, war=True)
        sy = self.synced[eng]
        for k, d in deps.items():
            if sy.get(k, -1) >= d.pos:
                continue
            op.waits.append(d)
            d.need_inc = True
            sy[k] = d.pos
        for r in reads:
            old = r.r.get(op.chan)
            if old is None or old.pos < op.pos:
                r.r[op.chan] = op
        for w in writes:
            w.w = op
            w.r = {}
        self.ops[eng].append(op)
        return op

    def finalize(self):
        for e in ENGS:
            t = 0
            for op in self.ops[e]:
                if op.is_dma:
                    op.ticket = op.pos * 16
                elif op.need_inc:
                    t += 1
                    op.ticket = t

    def run(self, nc, stack):
        self.finalize()
        for c in list(self.echan.values()) + self.dchans:
            c.handle = stack.enter_context(nc.semaphore("s_" + c.name))
        block = stack.enter_context(nc.Block())
        decos = {"sp": block.sync, "act": block.scalar, "dve": block.vector,
                 "pool": block.gpsimd, "pe": block.tensor}
        for eng in ENGS:
            ops = self.ops[eng]
            outs = self.out_ops if eng == "sp" else None

            def body(e, ops=ops, outs=outs):
                for op in ops:
                    for d in op.waits:
                        e.wait_ge(d.chan.handle, d.ticket)
                    ins = op.fn(e)
                    if op.is_dma:
                        ins.then_inc(op.chan.handle, 16)
                    elif op.need_inc:
                        ins.then_inc(op.chan.handle, 1)
                if outs is not None:
                    for ch in self.dchans:
                        if ch.count:
                            e.wait_ge(ch.handle, ch.count * 16)

            decos[eng](body)


def _rope_tables(pos):
    pos = np.asarray(pos, dtype=np.float64)
    S = pos.shape[0]
    out = np.zeros((4, 128, S), np.float32)
    f32 = (10000.0 ** (-(np.arange(0, 32, 2, dtype=np.float32) / np.float32(32)))).astype(np.float32)
    f16 = (500000.0 ** (-(np.arange(0, 16, 2, dtype=np.float32) / np.float32(16)))).astype(np.float32)
    row = np.floor(pos / 64.0).astype(np.float32)
    col = (pos - 64.0 * np.floor(pos / 64.0)).astype(np.float32)
    posf = pos.astype(np.float32)
    for p in range(128):
        d = p % 64
        if d < 32:
            ang = row * f32[d % 16]
        else:
            ang = col * f32[(d - 32) % 16]
        ang = ang.astype(np.float32)
        out[0, p] = np.cos(ang)
        out[1, p] = np.sin(ang)
        if d < 16:
            a2 = (posf * f16[d % 8]).astype(np.float32)
            out[2, p] = np.cos(a2)
            out[3, p] = np.sin(a2)
        else:
            out[2, p] = 1.0
            out[3, p] = 0.0
    return out


def _const_mats():
    cm = np.zeros((6, 128, 128), np.float32)
    cm[0] = np.eye(128)
    for b in range(2):
        cm[1, b * 64:(b + 1) * 64, b * 64:(b + 1) * 64] = 1.0 / 64.0
    cm[2] = 1.0 / 128.0
    cm[3] = 1.0
    for b in range(2):
        o = b * 64
        for blk in (0, 32):
            for i in range(16):
                cm[4, o + blk + i + 16, o + blk + i] = -1.0
                cm[4, o + blk + i, o + blk + i + 16] = 1.0
        for i in range(8):
            cm[5, o + i + 8, o + i] = -1.0
            cm[5, o + i, o + i + 8] = 1.0
    return np.ascontiguousarray(cm.transpose(1, 0, 2)).astype(ml_dtypes.bfloat16)


def _mask_table():
    kk = np.arange(128)[:, None]
    u = np.arange(MASK_W)[None, :]
    dl = u - kk - MASK_U0
    a = np.abs(dl)
    c = (a <= 64).astype(np.float32) + ((dl % 4 == 0) & (a <= 256)) + ((dl % 16 == 0) & (a <= 1024))
    return c.astype(ml_dtypes.bfloat16)


def build(cfg):
    NP = cfg["NP"]
    SP = cfg["SP"]
    has_s = cfg["sample"]
    SS, NQ0, NQ1 = cfg.get("SS", 4096), cfg.get("NQ0", 3072), cfg.get("NQ1", 2048)

    nc = bass.Bass("TRN2", target_bir_lowering=False)

    def dram(name, shape, dt, kind):
        return nc.dram_tensor(name, list(shape), dt, kind=kind).ap()

    EI, EO, IN = "ExternalInput", "ExternalOutput", "Internal"
    jobs = []
    if NP:
        xp = dram("xp", [NP, SP, D], F32, EI)
        yp = dram("yp", [NP, SP, D], F32, EO)
        ropeP = dram("ropeP", [4, 128, SP], F32, EI)
        x1p = dram("x1p", [NP, SP, D], F32, IN)
        for i in range(NP):
            jobs.append(dict(name=f"p{i}", x=xp[i], y=yp[i], x1=x1p[i], rope=ropeP, S=SP, nq0=SP, nq1=SP))
    if has_s:
        xs = dram("xs", [SS, D], F32, EI)
        ys = dram("ys", [NQ1, D], F32, EO)
        ropeS = dram("ropeS", [4, 128, SS], F32, EI)
        x1s = dram("x1s", [NQ0, D], F32, IN)
        jobs.append(dict(name="s", x=xs, y=ys, x1=x1s, rope=ropeS, S=SS, nq0=NQ0, nq1=NQ1))
    w_in0 = dram("w_in0", [D, 3328], F32, EI)
    w_out0 = dram("w_out0", [D, D], F32, EI)
    w_in1 = dram("w_in1", [D, 4096], F32, EI)
    w_out1 = dram("w_out1", [D, D], F32, EI)
    vec64 = dram("vec64", [10, 64], F32, EI)
    subln = dram("subln", [128], F32, EI)
    norms = dram("norms", [2, D], F32, EI)
    cmat = dram("cmat", [128, 6, 128], BF16, EI)
    maskd = dram("maskd", [128, MASK_W], BF16, EI)
    wsc0 = dram("wsc0", [128, 8, 3328], BF16, IN)
    wso0 = dram("wso0", [128, 8, D], BF16, IN)
    wsc1 = dram("wsc1", [128, 8, 4096], BF16, IN)
    wso1 = dram("wso1", [128, 8, D], BF16, IN)

    KVC = 0
    for j in jobs:
        S = j["S"]
        l0 = 5 * S + (S // 128) * 256 + (S // 128) * 512
        l1 = 8 * j["nq0"] + (j["nq0"] // 128) * 1024
        KVC = max(KVC, l0, l1)
    KVC = max(KVC, 2 * 8192 + 2 * 4096, cfg.get('KVC_MIN', 0))

    P = Prog()
    stack = ExitStack()
    with stack:
        def sb(name, shape, dt):
            return stack.enter_context(nc.sbuf_tensor(name, list(shape), dt))

        KV = sb("KV", [128, KVC], BF16)
        hT = sb("hT", [128, 8, 512], BF16)
        QT = sb("QT", [128, 8, 512], BF16)
        GT = sb("GT", [128, 8, 512], BF16)
        xin = sb("xin", [128, 2, D], F32)
        xnb = sb("xnb", [128, 2, D], BF16)
        W = [sb("W0", [128, 8, D], BF16), sb("W1", [128, 8, D], BF16)]
        rope = sb("rope", [128, 4, 512], F32)
        PT = sb("PT", [128, 6, 512], BF16)
        PM = sb("PM", [128, 6, 512], BF16)
        mask = sb("mask", [128, MASK_W], BF16)
        sq = sb("sq", [128, 2, 512], BF16)
        zb = sb("zb", [128, 2, 512], BF16)
        lnr = sb("lnr", [128, 2, 512], F32)
        t1 = sb("t1", [128, 2, 512], F32)
        t2 = sb("t2", [128, 512], F32)
        cm = sb("cm", [128, 6, 128], BF16)
        G = sb("G", [128, 12], F32)
        normT = sb("normT", [128, 16], F32)
        ssx = sb("ssx", [128, 8], F32)
        lam4 = sb("lam4", [128, 4, 64], F32)
        psb = [stack.enter_context(nc.psum_tensor(f"ps{i}", [128, 512], F32)) for i in range(8)]
        TPb = psb[6][:, :].bitcast(BF16)

        ident = cm[:, 0, :]
        bones64 = cm[:, 1, :]
        bones128 = cm[:, 2, :]
        ones = cm[:, 3, :]
        RA = cm[:, 4, :]
        RP = cm[:, 5, :]

        r_KV = {}

        def rkv(key):
            if key not in r_KV:
                r_KV[key] = Res("kv%s" % (key,))
            return r_KV[key]

        r_hm = [[Res(f"hm{c}_{s}") for s in range(4)] for c in range(8)]
        r_QT = [Res(f"qt{j}") for j in range(8)]
        r_GT = [Res(f"gt{j}") for j in range(8)]
        r_xin = [Res("xin0"), Res("xin1")]
        r_xnb = [Res("xnb0"), Res("xnb1")]
        r_W = [Res("W0"), Res("W1")]
        r_rope = Res("rope")
        r_PT = [Res(f"pt{i}") for i in range(6)]
        r_PM = [Res(f"pm{i}") for i in range(6)]
        r_mask = Res("mask")
        r_sq = [Res("sq0"), Res("sq1")]
        r_zb = [Res("zb0"), Res("zb1")]
        r_lnr = [Res("lnr0"), Res("lnr1")]
        r_t1 = [Res("t10"), Res("t11")]
        r_t2 = Res("t2")
        r_cm = Res("cm")
        r_G = Res("G")
        r_normT = Res("normT")
        r_ssx = Res("ssx")
        r_lam4 = Res("lam4")
        r_ps = [Res(f"ps{i}", excl=True) for i in range(8)]
        r_wsc = {k: [Res(k + "a"), Res(k + "b")] for k in ("wsc0", "wso0", "wsc1", "wso1")}
        r_kvstage = Res("kvstage")

        ch_xin = [P.dchan("dxin0"), P.dchan("dxin1")]
        ch_W = [P.dchan("dW0"), P.dchan("dW1")]
        ch_rope = P.dchan("drope")
        ch_const = P.dchan("dconst")
        ch_st = [P.dchan("dstore0"), P.dchan("dstore1")]
        ch_out = [P.dchan("dout0"), P.dchan("dout1")]
        ch_wst = [P.dchan("dwst0"), P.dchan("dwst1")]
        ch_wsto = [P.dchan("dwsto0"), P.dchan("dwsto1")]

        P.emit("sp", lambda e: e.dma_start(out=cm[:, :, :], in_=cmat), writes=[r_cm], dma=ch_const)
        P.emit("sp", lambda e: e.dma_start(out=mask[:, :], in_=maskd), writes=[r_mask], dma=ch_const)
        for i in range(6):
            for hb in range(2):
                P.emit("sp", lambda e, i=i, hb=hb: e.dma_start(
                    out=G[hb * 64:(hb + 1) * 64, i:i + 1], in_=vec64[i].rearrange("(p o) -> p o", o=1)),
                    writes=[r_G], dma=ch_const)
        P.emit("sp", lambda e: e.dma_start(out=G[:, 6:7], in_=subln.rearrange("(p o) -> p o", o=1)),
               writes=[r_G], dma=ch_const)
        for l in range(2):
            for c in range(8):
                P.emit("sp", lambda e, l=l, c=c: e.dma_start(
                    out=normT[:, l * 8 + c:l * 8 + c + 1],
                    in_=norms[l, c * 128:(c + 1) * 128].rearrange("(p o) -> p o", o=1)),
                    writes=[r_normT], dma=ch_const)
        for i in range(4):
            P.emit("sp", lambda e, i=i: e.dma_start(out=lam4[:, i, :], in_=vec64[6 + i].partition_broadcast(128)),
                   writes=[r_lam4], dma=ch_const)
        last_const = P.ops["sp"][-1]
        for rr in (r_cm, r_mask, r_G, r_normT, r_lam4):
            rr.w = last_const
        P.emit("dve", lambda e: e.scalar_tensor_tensor(out=t2[:, 0:64], in0=lam4[:, 0, :], scalar=1.0, in1=lam4[:, 1, :],
                                                       op0=ALU.mult, op1=ALU.mult, accum_out=ssx[:, 4:5]),
               reads=[r_lam4], writes=[r_t2, r_ssx])
        P.emit("dve", lambda e: e.scalar_tensor_tensor(out=t2[:, 64:128], in0=lam4[:, 2, :], scalar=1.0, in1=lam4[:, 3, :],
                                                       op0=ALU.mult, op1=ALU.mult, accum_out=ssx[:, 5:6]),
               reads=[r_lam4], writes=[r_t2, r_ssx])
        P.emit("act", lambda e: e.activation(out=ssx[:, 6:8], in_=ssx[:, 4:6], func=AF.Exp), reads=[r_ssx], writes=[r_ssx])
        P.emit("dve", lambda e: e.tensor_tensor(out=ssx[:, 4:5], in0=ssx[:, 7:8], in1=ssx[:, 6:7], op=ALU.subtract),
               reads=[r_ssx], writes=[r_ssx])
        P.emit("dve", lambda e: e.tensor_scalar(out=G[:, 7:8], in0=ssx[:, 4:5], scalar1=-0.2, scalar2=None, op0=ALU.add),
               reads=[r_ssx, r_G], writes=[r_G])
        P.emit("dve", lambda e: e.tensor_scalar(out=G[:, 6:7], in0=G[:, 6:7], scalar1=0.8, scalar2=None, op0=ALU.mult),
               reads=[r_G], writes=[r_G])
        P.emit("dve", lambda e: e.memset(G[:, 8:9], EPS), reads=[], writes=[r_G])

        stage_f = [KV[:, 0:8192].bitcast(F32), KV[:, 8192:16384].bitcast(F32)]
        stage_b = [KV[:, 16384:20480], KV[:, 20480:24576]]
        r_sf = [Res("sf0"), Res("sf1")]
        r_sbb = [Res("sb0"), Res("sb1")]
        wi = 0
        for (src, dst, rdst, ncols, nl) in ((w_in0, wsc0, "wsc0", 3328, 0), (w_out0, wso0, "wso0", D, None),
                                            (w_in1, wsc1, "wsc1", 4096, 1), (w_out1, wso1, "wso1", D, None)):
            for c in range(8):
                s = wi % 2
                wi += 1
                P.emit("sp", lambda e, s=s, c=c, src=src, ncols=ncols: e.dma_start(
                    out=stage_f[s][:, 0:ncols], in_=src[c * 128:(c + 1) * 128, :]),
                    writes=[r_sf[s]], dma=ch_wst[s])
                eng = "dve" if s == 0 else "pool"
                if nl is None:
                    P.emit(eng, lambda e, s=s, ncols=ncols: e.tensor_copy(out=stage_b[s][:, 0:ncols], in_=stage_f[s][:, 0:ncols]),
                           reads=[r_sf[s]], writes=[r_sbb[s]])
                else:
                    P.emit(eng, lambda e, s=s, ncols=ncols, col=nl * 8 + c: e.tensor_scalar(
                        out=stage_b[s][:, 0:ncols], in0=stage_f[s][:, 0:ncols], scalar1=normT[:, col:col + 1],
                        scalar2=None, op0=ALU.mult),
                        reads=[r_sf[s], r_normT], writes=[r_sbb[s]])
                P.emit("sp", lambda e, s=s, c=c, dst=dst, ncols=ncols: e.dma_start(out=dst[:, c, :], in_=stage_b[s][:, 0:ncols]),
                       reads=[r_sbb[s]], writes=[r_wsc[rdst][s]], dma=ch_wsto[s])
        kv_all_boundary = [r_sf[0], r_sf[1], r_sbb[0], r_sbb[1]]

        counters = dict(xs=0, qk=0, pt=0, sb=0)
        fscr = sb("fscr", [128, 4], F32)

        def fence(res_list):
            for k, eng in enumerate(("pool", "dve")):
                P.emit(eng, lambda e, k=k: e.memset(fscr[:, k:k + 1], 0.0), reads=[], writes=list(res_list))

        def load_w(slot, wname, wd, col0, n):
            for h in range(2):
                P.emit("sp", lambda e, h=h: e.dma_start(out=W[slot][:, h * 4:(h + 1) * 4, 0:n],
                                                        in_=wd[:, h * 4:(h + 1) * 4, col0:col0 + n]),
                       reads=r_wsc[wname], writes=[r_W[slot]], dma=ch_W[slot])

        cur = dict(h=hT, hres=r_hm)

        def norm_tile(src, src_res, tok0):
            hdst, hres = cur["h"], cur["hres"]
            for s in range(4):
                k = counters["xs"]
                counters["xs"] += 1
                sl = k % 2
                t0 = tok0 + s * 128
                P.emit("sp", lambda e, sl=sl, t0=t0: e.dma_start(out=xin[:, sl, :], in_=src[t0:t0 + 128, :]),
                       reads=src_res, writes=[r_xin[sl]], dma=ch_xin[sl])
                P.emit("act", lambda e, sl=sl: e.activation(out=xnb[:, sl, :], in_=xin[:, sl, :], func=AF.Square,
                                                            accum_out=ssx[:, sl:sl + 1]),
                       reads=[r_xin[sl]], writes=[r_xnb[sl], r_ssx])
                P.emit("act", lambda e, sl=sl: e.activation(out=ssx[:, 2 + sl:3 + sl], in_=ssx[:, sl:sl + 1], func=AF.Ln,
                                                            scale=1.0 / D, bias=G[:, 8:9]),
                       reads=[r_ssx, r_G], writes=[r_ssx])
                P.emit("act", lambda e, sl=sl: e.activation(out=ssx[:, 2 + sl:3 + sl], in_=ssx[:, 2 + sl:3 + sl], func=AF.Exp,
                                                            scale=-0.5),
                       reads=[r_ssx], writes=[r_ssx])
                P.emit("dve", lambda e, sl=sl: e.tensor_scalar(out=xnb[:, sl, :], in0=xin[:, sl, :],
                                                               scalar1=ssx[:, 2 + sl:3 + sl], scalar2=None, op0=ALU.mult),
                       reads=[r_xin[sl], r_ssx], writes=[r_xnb[sl]])
                for c in range(8):
                    P.emit("pe", lambda e, sl=sl, c=c: e.transpose(out=TPb[:, c * 128:(c + 1) * 128],
                                                                   in_=xnb[:, sl, c * 128:(c + 1) * 128], identity=ident),
                           reads=[r_xnb[sl], r_cm], writes=[r_ps[6]])
                P.emit("dve", lambda e, s=s: e.tensor_copy(out=hdst[:, :, s * 128:(s + 1) * 128],
                                                           in_=TPb.rearrange("p (c t) -> p c t", c=8)),
                       reads=[r_ps[6]], writes=[hres[c][s] for c in range(8)])

        def load_rope(job, tok0):
            P.emit("sp", lambda e: e.dma_start(out=rope[:, :, :],
                                               in_=job["rope"][:, :, tok0:tok0 + 512].rearrange("t p s -> p t s")),
                   writes=[r_rope], dma=ch_rope)

        def proj_fm(slot, col0, bank):
            hsrc, hres = cur["h"], cur["hres"]
            for c in range(8):
                P.emit("pe", lambda e, c=c: e.matmul(psb[bank][:, :], lhsT=W[slot][:, c, col0:col0 + 128], rhs=hsrc[:, c, :],
                                                     start=(c == 0), stop=(c == 7)),
                       reads=[r_W[slot]] + hres[c], writes=[r_ps[bank]])

        def qk_tile(slot, col0, dest, dest_res, gcol, rtype):
            i = counters["qk"]
            counters["qk"] += 1
            b = i % 2
            zbk, ssbk, rotbk = b, 2 + b, 4 + b
            ci, si, Rm = (0, 1, RA) if rtype == "A" else (2, 3, RP)
            if i == 0: chk(1.505)
            proj_fm(slot, col0, zbk)
            if i == 0: chk(1.51)
            P.emit("act", lambda e: e.activation(out=sq[:, b, :], in_=psb[zbk][:, :], func=AF.Square),
                   reads=[r_ps[zbk]], writes=[r_sq[b]])
            if i == 0: chk(1.52)
            P.emit("act", lambda e: e.activation(out=zb[:, b, :], in_=psb[zbk][:, :], func=AF.Identity,
                                                 scale=G[:, gcol:gcol + 1]),
                   reads=[r_ps[zbk], r_G], writes=[r_zb[b]])
            if i == 0: chk(1.53)
            P.emit("pe", lambda e: e.matmul(psb[ssbk][:, :], lhsT=bones64, rhs=sq[:, b, :], start=True, stop=True),
                   reads=[r_sq[b], r_cm], writes=[r_ps[ssbk]])
            P.emit("pe", lambda e: e.matmul(psb[rotbk][:, :], lhsT=Rm, rhs=zb[:, b, :], start=True, stop=True),
                   reads=[r_zb[b], r_cm], writes=[r_ps[rotbk]])
            if i == 0: chk(1.54)
            P.emit("act", lambda e: e.activation(out=lnr[:, b, :], in_=psb[ssbk][:, :], func=AF.Ln, bias=G[:, 8:9]),
                   reads=[r_ps[ssbk], r_G], writes=[r_lnr[b]])
            P.emit("act", lambda e: e.activation(out=lnr[:, b, :], in_=lnr[:, b, :], func=AF.Exp, scale=-0.5),
                   reads=[r_lnr[b]], writes=[r_lnr[b]])
            if i == 0: chk(1.55)
            P.emit("dve", lambda e: e.scalar_tensor_tensor(out=t1[:, b, :], in0=rope[:, ci, :], scalar=G[:, gcol:gcol + 1],
                                                           in1=psb[zbk][:, :], op0=ALU.mult, op1=ALU.mult),
                   reads=[r_ps[zbk], r_G, r_rope], writes=[r_t1[b]])
            if i == 0: chk(1.56)
            P.emit("dve", lambda e: e.tensor_tensor(out=t2[:, :], in0=psb[rotbk][:, :], in1=rope[:, si, :], op=ALU.mult),
                   reads=[r_ps[rotbk], r_rope], writes=[r_t2])
            if i == 0: chk(1.57)
            P.emit("dve", lambda e: e.tensor_tensor(out=t1[:, b, :], in0=t1[:, b, :], in1=t2[:, :], op=ALU.add),
                   reads=[r_t1[b], r_t2], writes=[r_t1[b]])
            if i == 0: chk(1.58)
            P.emit("pool", lambda e: e.tensor_tensor(out=dest, in0=t1[:, b, :], in1=lnr[:, b, :], op=ALU.mult),
                   reads=[r_t1[b], r_lnr[b]], writes=dest_res)

        def gate_tile(slot, col0, j):
            i = counters["qk"]
            counters["qk"] += 1
            b = i % 2
            proj_fm(slot, col0, b)
            P.emit("act", lambda e: e.activation(out=GT[:, j, :], in_=psb[b][:, :], func=AF.Silu),
                   reads=[r_ps[b]], writes=[r_GT[j]])

        def v_tiles(slot, ncols, evac):
            hsrc, hres = cur["h"], cur["hres"]
            for s in range(4):
                nb = (ncols + 511) // 512
                for n in range(nb):
                    wd = min(512, ncols - n * 512)
                    for c in range(8):
                        P.emit("pe", lambda e, c=c, n=n, wd=wd, s=s: e.matmul(
                            psb[n][:, 0:wd], lhsT=hsrc[:, c, s * 128:(s + 1) * 128], rhs=W[slot][:, c, n * 512:n * 512 + wd],
                            start=(c == 0), stop=(c == 7)),
                            reads=[r_W[slot], hres[c][s]], writes=[r_ps[n]])
                    evac(s, n, n, wd)

        def out_proj(src, src_res, dst, dst_res, tok0, slot, dchan, is_out):
            sls = []

            def ld(s):
                k = counters["xs"]
                counters["xs"] += 1
                sl = k % 2
                sls.append(sl)
                t0 = tok0 + s * 128
                P.emit("sp", lambda e: e.dma_start(out=xin[:, sl, :], in_=src[t0:t0 + 128, :]),
                       reads=src_res, writes=[r_xin[sl]], dma=ch_xin[sl])

            ld(0)
            ld(1)
            for s in range(4):
                sl = sls[s]
                t0 = tok0 + s * 128
                for n in range(2):
                    bank = 4 + n
                    for c in range(8):
                        P.emit("pe", lambda e, c=c, n=n, s=s, bank=bank: e.matmul(
                            psb[bank][:, :], lhsT=hT[:, c, s * 128:(s + 1) * 128], rhs=W[slot][:, c, n * 512:(n + 1) * 512],
                            start=(c == 0), stop=(c == 7)),
                            reads=[r_W[slot], r_hm[c][s]], writes=[r_ps[bank]])
                    P.emit("dve", lambda e, n=n, sl=sl, bank=bank: e.tensor_tensor(
                        out=xin[:, sl, n * 512:(n + 1) * 512], in0=psb[bank][:, :], in1=xin[:, sl, n * 512:(n + 1) * 512],
                        op=ALU.add),
                        reads=[r_ps[bank], r_xin[sl]], writes=[r_xin[sl]])
                op = P.emit("sp", lambda e, sl=sl, t0=t0: e.dma_start(out=dst[t0:t0 + 128, :], in_=xin[:, sl, :]),
                            reads=[r_xin[sl]], writes=[dst_res[sl]], dma=dchan[sl])
                if is_out:
                    P.out_ops.append(op)
                if s + 2 < 4:
                    ld(s + 2)

        pending = []

        def flush_pending():
            while pending:
                pending.pop(0)()

        def attn_pairs(kts, s_mm, pv_mm, masked, qs, sbanks=(0, 1, 2, 3)):
            kts = list(kts)
            n = len(kts)
            nr = len(sbanks)
            depth = nr // 2
            slots = {}
            cnt = [0]

            def issue_s(ii):
                kt = kts[ii]
                for m in range(2):
                    idx = cnt[0] % nr
                    cnt[0] += 1
                    r = sbanks[idx]
                    a = idx
                    fn, reads = s_mm(kt, m, r)
                    P.emit("pe", fn, reads=reads, writes=[r_ps[r]])
                    P.emit("act", lambda e, a=a, r=r: e.activation(out=PT[:, a, :], in_=psb[r][:, :], func=AF.Exp, scale=0.125),
                           reads=[r_ps[r]], writes=[r_PT[a]])
                    if masked:
                        off = qs - 128 * kt + MASK_U0
                        P.emit("dve", lambda e, a=a, off=off: e.tensor_tensor(out=PM[:, a, :], in0=PT[:, a, :],
                                                                              in1=mask[:, off:off + 512], op=ALU.mult),
                               reads=[r_PT[a], r_mask], writes=[r_PM[a]])
                        slots[(ii, m)] = (PM[:, a, :], r_PM[a])
                    else:
                        slots[(ii, m)] = (PT[:, a, :], r_PT[a])

            def issue_pv(ii):
                kt = kts[ii]
                for m in range(2):
                    src, rsrc = slots.pop((ii, m))
                    for fn, reads, writes in pv_mm(kt, m, src, ii == 0, ii == n - 1):
                        P.emit("pe", fn, reads=reads + [rsrc], writes=writes)

            for ii in range(min(depth - 1, n)):
                issue_s(ii)
            for ii in range(n):
                if ii + depth - 1 < n:
                    issue_s(ii + depth - 1)
                issue_pv(ii)
                if ii == min(5, n - 1):
                    flush_pending()

        def mix_out(j, pre, pre_res, eng="pool"):
            P.emit(eng, lambda e: e.tensor_tensor(out=hT[:, j, :], in0=pre, in1=GT[:, j, :], op=ALU.mult),
                   reads=[pre_res, r_GT[j]], writes=r_hm[j])

        kv_prev = [r_sf[0], r_sf[1], r_sbb[0], r_sbb[1]]
        STOP = cfg.get("stop", 99)

        class _Stop(Exception):
            pass

        def chk(level):
            if STOP <= level:
                raise _Stop()

        try:
          chk(0)
          for job in jobs:
              S = job["S"]
              nq0, nq1 = job["nq0"], job["nq1"]
              tag = job["name"]
              r_x1 = [Res("x1a" + tag), Res("x1b" + tag)]
              r_y = [Res("ya" + tag), Res("yb" + tag)]
              for L in range(2):
                  if L == 0:
                      n_kv, n_q = S, nq0
                      src, src_res, dst, dst_res, dchan, is_out = job["x"], [], job["x1"], r_x1, ch_st, False
                      wsc, wname, wso, woname = wsc0, "wsc0", wso0, "wso0"
                      kc0, kn, vc0, vn, qc0, gc0 = 0, 640, 640, 640, 1280, 2304
                  else:
                      n_kv, n_q = nq0, nq1
                      src, src_res, dst, dst_res, dchan, is_out = job["x1"], r_x1, job["y"], r_y, ch_out, True
                      wsc, wname, wso, woname = wsc1, "wsc1", wso1, "wso1"
                      kc0, kn, vc0, vn, qc0, gc0 = 0, 1024, 1024, 1024, 2048, 3072
                  nkt = n_kv // 128
                  ntt = n_kv // 512
                  KA = 0
                  KB = [n_kv + h * n_kv for h in range(4)]
                  VA = 5 * n_kv
                  VB = VA + nkt * 256
                  KC = [jj * n_kv for jj in range(8)]
                  VC = 8 * n_kv
                  rk = [Res(f"k{tag}{L}_{tt}") for tt in range(ntt)]
                  rv = [Res(f"v{tag}{L}_{tt}") for tt in range(ntt)]
                  r_ones = Res(f"ones{tag}{L}")

                  load_w(0, wname, wsc, kc0, kn)
                  load_w(1, wname, wsc, vc0, vn)
                  chk(1 + 10 * L)
                  fence(kv_prev)
                  kv_prev = rk + rv + [r_ones]
                  kv_end = (VB + nkt * 512) if L == 0 else (VC + nkt * 1024)
                  use_hall = (kv_end + 8 * n_kv) <= KVC and not cfg.get("no_hall", False)
                  if use_hall:
                      hall3 = KV[:, kv_end:kv_end + 8 * n_kv].rearrange("p (c t) -> p c t", c=8)
                      r_hall = [[[Res(f"hall{tag}{L}_{tt}_{c}_{s_}") for s_ in range(4)] for c in range(8)] for tt in range(ntt)]
                      for tt_ in range(ntt):
                          for c_ in range(8):
                              kv_prev = kv_prev + r_hall[tt_][c_]
                  if L == 0:
                      P.emit("pool", lambda e, VA=VA, nkt=nkt: e.memset(
                          KV[:, VA:VA + nkt * 256].rearrange("p (k c) -> p k c", c=256)[:, :, 64:192], 1.0),
                          reads=[], writes=[r_ones])
                  for tt in range(ntt):
                      chk(1.2 + 10 * L)
                      if use_hall:
                          cur["h"], cur["hres"] = hall3[:, :, tt * 512:(tt + 1) * 512], r_hall[tt]
                      else:
                          cur["h"], cur["hres"] = hT, r_hm
                      norm_tile(src, src_res, tt * 512)
                      chk(1.4 + 10 * L)
                      load_rope(job, tt * 512)
                      if L == 0:
                          qk_tile(0, 0, KV[:, KA + tt * 512:KA + (tt + 1) * 512], [rk[tt]], 1, "A")
                          chk(1.6)
                          for h in range(4):
                              qk_tile(0, 128 + h * 128, KV[:, KB[h] + tt * 512:KB[h] + (tt + 1) * 512], [rk[tt]], 3, "P")

                          def evac(s, n, bank, wd, tt=tt, VA=VA, VB=VB, rv=rv):
                              kt = tt * 4 + s
                              if n == 0:
                                  P.emit("dve", lambda e: e.tensor_copy(out=KV[:, VB + kt * 512:VB + (kt + 1) * 512],
                                                                        in_=psb[bank][:, :]),
                                         reads=[r_ps[bank]], writes=[rv[tt]])
                              else:
                                  for a in range(2):
                                      P.emit("dve", lambda e, a=a: e.tensor_copy(
                                          out=KV[:, VA + kt * 256 + a * 192:VA + kt * 256 + a * 192 + 64],
                                          in_=psb[bank][:, a * 64:(a + 1) * 64]),
                                          reads=[r_ps[bank]], writes=[rv[tt]])
                      else:
                          for jj in range(8):
                              qk_tile(0, jj * 128, KV[:, KC[jj] + tt * 512:KC[jj] + (tt + 1) * 512], [rk[tt]], 5, "P")

                          def evac(s, n, bank, wd, tt=tt, VC=VC, rv=rv):
                              kt = tt * 4 + s
                              P.emit("dve", lambda e: e.tensor_copy(
                                  out=KV[:, VC + kt * 1024 + n * 512:VC + kt * 1024 + (n + 1) * 512], in_=psb[bank][:, :]),
                                  reads=[r_ps[bank]], writes=[rv[tt]])
                      chk(1.8 + 10 * L)
                      v_tiles(1, vn, evac)

                  chk(2 + 10 * L)
                  load_w(1, wname, wsc, gc0, 1024)
                  load_w(0, wname, wsc, qc0, 1024)
                  for g in range(n_q // 512):
                      qs = g * 512
                      if use_hall:
                          cur["h"], cur["hres"] = hall3[:, :, qs:qs + 512], r_hall[g]
                      else:
                          cur["h"], cur["hres"] = hT, r_hm
                          norm_tile(src, src_res, qs)
                      load_rope(job, qs)
                      if g > 0:
                          load_w(0, wname, wsc, qc0, 1024)
                      for j in range(8):
                          gate_tile(1, j * 128, j)
                      if L == 0:
                          for j in range(4):
                              qk_tile(0, j * 128, QT[:, j, :], [r_QT[j]], 0, "A")
                          for j in range(4, 8):
                              qk_tile(0, j * 128, QT[:, j, :], [r_QT[j]], 2, "P")
                      else:
                          for j in range(8):
                              qk_tile(0, j * 128, QT[:, j, :], [r_QT[j]], 4, "P")
                      load_w(0, woname, wso, 0, 1024)
                      if L == 0:
                          chk(3)
                          for j in range(4):
                              def s_mm(kt, m, r, j=j):
                                  rows = slice(m * 64, (m + 1) * 64)
                                  lhs = KV[rows, KA + kt * 128:KA + (kt + 1) * 128]
                                  rhs = QT[rows, j, :]
                                  return (lambda e: e.matmul(psb[r][:, :], lhsT=lhs, rhs=rhs, start=True, stop=True),
                                          [rk[kt // 4], r_QT[j]])

                              def pv_mm(kt, m, srcp, first, last):
                                  c0 = VA + kt * 256 + m * 128
                                  bank = 4 + m
                                  return [(lambda e: e.matmul(psb[bank][:, :], lhsT=KV[:, c0:c0 + 128], rhs=srcp,
                                                              start=first, stop=last),
                                           [rv[kt // 4], r_ones], [r_ps[bank]])]

                              attn_pairs(range(nkt), s_mm, pv_mm, False, 0, sbanks=(0, 1, 2, 3, 6, 7))
                              P.emit("dve", lambda e: e.tensor_copy(out=t1[:, 0, :], in_=psb[4][:, :]), reads=[r_ps[4]], writes=[r_t1[0]])
                              P.emit("dve", lambda e: e.tensor_copy(out=t1[:, 1, :], in_=psb[5][:, :]), reads=[r_ps[5]], writes=[r_t1[1]])

                              def epiA(j=j):
                                  P.emit("dve", lambda e: e.reciprocal(out=lnr[0:64, 0, :], in_=t1[64:128, 0, :]),
                                         reads=[r_t1[0]], writes=[r_lnr[0]])
                                  P.emit("dve", lambda e: e.reciprocal(out=lnr[64:128, 0, :], in_=t1[0:64, 1, :]),
                                         reads=[r_t1[1]], writes=[r_lnr[0]])
                                  P.emit("pool", lambda e: e.tensor_tensor(out=t2[0:64, :], in0=t1[0:64, 0, :], in1=lnr[0:64, 0, :], op=ALU.mult),
                                         reads=[r_t1[0], r_lnr[0]], writes=[r_t2])
                                  P.emit("pool", lambda e: e.tensor_tensor(out=t2[64:128, :], in0=t1[64:128, 1, :], in1=lnr[64:128, 0, :], op=ALU.mult),
                                         reads=[r_t1[1], r_lnr[0]], writes=[r_t2])
                                  mix_out(j, t2[:, :], r_t2)

                              pending.append(epiA)
                          chk(4)
                          for h in range(4):
                              j = 4 + h

                              def s_mm(kt, m, r, j=j, h=h):
                                  rows = slice(m * 64, (m + 1) * 64)
                                  lhs = KV[rows, KB[h] + kt * 128:KB[h] + (kt + 1) * 128]
                                  rhs = QT[rows, j, :]
                                  return (lambda e: e.matmul(psb[r][:, :], lhsT=lhs, rhs=rhs, start=True, stop=True),
                                          [rk[kt // 4], r_QT[j]])

                              def pv_mm(kt, m, srcp, first, last, h=h):
                                  c0 = VB + kt * 512 + h * 128
                                  bo, bd = 4 + 2 * m, 5 + 2 * m
                                  return [(lambda e: e.matmul(psb[bo][:, :], lhsT=KV[:, c0:c0 + 128], rhs=srcp,
                                                              start=first, stop=last),
                                           [rv[kt // 4]], [r_ps[bo]]),
                                          (lambda e: e.matmul(psb[bd][:, :], lhsT=ones, rhs=srcp, start=first, stop=last),
                                           [r_cm], [r_ps[bd]])]

                              attn_pairs(range(nkt), s_mm, pv_mm, False, 0)
                              P.emit("dve", lambda e: e.tensor_copy(out=t1[:, 0, :], in_=psb[4][:, :]), reads=[r_ps[4]], writes=[r_t1[0]])
                              P.emit("act", lambda e: e.activation(out=lnr[:, 0, :], in_=psb[5][:, :], func=AF.Ln),
                                     reads=[r_ps[5]], writes=[r_lnr[0]])
                              P.emit("dve", lambda e: e.tensor_copy(out=t1[:, 1, :], in_=psb[6][:, :]), reads=[r_ps[6]], writes=[r_t1[1]])
                              P.emit("act", lambda e: e.activation(out=lnr[:, 1, :], in_=psb[7][:, :], func=AF.Ln),
                                     reads=[r_ps[7]], writes=[r_lnr[1]])

                              def epiB(j=j):
                                  for m in range(2):
                                      P.emit("act", lambda e, m=m: e.activation(out=lnr[:, m, :], in_=lnr[:, m, :], func=AF.Exp, scale=-1.0),
                                             reads=[r_lnr[m]], writes=[r_lnr[m]])
                                      P.emit("pool", lambda e, m=m: e.tensor_tensor(out=t1[:, m, :], in0=t1[:, m, :], in1=lnr[:, m, :], op=ALU.mult),
                                             reads=[r_t1[m], r_lnr[m]], writes=[r_t1[m]])
                                  P.emit("dve", lambda e: e.scalar_tensor_tensor(out=t2[:, :], in0=t1[:, 1, :], scalar=G[:, 7:8],
                                                                                 in1=t1[:, 0, :], op0=ALU.mult, op1=ALU.add),
                                         reads=[r_t1[0], r_t1[1], r_G], writes=[r_t2])
                                  P.emit("act", lambda e: e.activation(out=sq[:, 0, :], in_=t2[:, :], func=AF.Square),
                                         reads=[r_t2], writes=[r_sq[0]])
                                  rb = 0
                                  counters["sb"] += 1
                                  P.emit("pe", lambda e: e.matmul(psb[rb][:, :], lhsT=bones128, rhs=sq[:, 0, :], start=True, stop=True),
                                         reads=[r_sq[0], r_cm], writes=[r_ps[rb]])
                                  P.emit("act", lambda e: e.activation(out=lnr[:, 0, :], in_=psb[rb][:, :], func=AF.Ln, bias=G[:, 8:9]),
                                         reads=[r_ps[rb], r_G], writes=[r_lnr[0]])
                                  P.emit("act", lambda e: e.activation(out=lnr[:, 0, :], in_=lnr[:, 0, :], func=AF.Exp, scale=-0.5),
                                         reads=[r_lnr[0]], writes=[r_lnr[0]])
                                  P.emit("dve", lambda e: e.scalar_tensor_tensor(out=t1[:, 0, :], in0=t2[:, :], scalar=G[:, 6:7],
                                                                                 in1=lnr[:, 0, :], op0=ALU.mult, op1=ALU.mult),
                                         reads=[r_t2, r_G, r_lnr[0]], writes=[r_t1[0]])
                                  mix_out(j, t1[:, 0, :], r_t1[0])

                              pending.append(epiB)
                      else:
                          chk(13)
                          kt_lo = max(0, (qs - 1024) // 128)
                          kt_hi = min(nkt, (qs + 512 + 1024) // 128)
                          for j in range(8):
                              def s_mm(kt, m, r, j=j):
                                  rows = slice(m * 64, (m + 1) * 64)
                                  lhs = KV[rows, KC[j] + kt * 128:KC[j] + (kt + 1) * 128]
                                  rhs = QT[rows, j, :]
                                  return (lambda e: e.matmul(psb[r][:, :], lhsT=lhs, rhs=rhs, start=True, stop=True),
                                          [rk[kt // 4], r_QT[j]])

                              def pv_mm(kt, m, srcp, first, last, j=j):
                                  c0 = VC + kt * 1024 + (2 * j + m) * 64
                                  rows = slice(m * 64, (m + 1) * 64)
                                  return [(lambda e: e.matmul(psb[4][rows, :], lhsT=KV[:, c0:c0 + 64], rhs=srcp,
                                                              start=first, stop=last),
                                           [rv[kt // 4]], [r_ps[4]]),
                                          (lambda e: e.matmul(psb[5][rows, :], lhsT=ones[:, 0:64], rhs=srcp,
                                                              start=first, stop=last),
                                           [r_cm], [r_ps[5]])]

                              attn_pairs(range(kt_lo, kt_hi), s_mm, pv_mm, True, qs, sbanks=(0, 1, 2, 3, 6, 7))
                              P.emit("dve", lambda e: e.tensor_copy(out=t1[:, 0, :], in_=psb[4][:, :]), reads=[r_ps[4]], writes=[r_t1[0]])
                              P.emit("act", lambda e: e.activation(out=lnr[:, 0, :], in_=psb[5][:, :], func=AF.Ln),
                                     reads=[r_ps[5]], writes=[r_lnr[0]])

                              def epiC(j=j):
                                  P.emit("act", lambda e: e.activation(out=lnr[:, 0, :], in_=lnr[:, 0, :], func=AF.Exp, scale=-1.0),
                                         reads=[r_lnr[0]], writes=[r_lnr[0]])
                                  P.emit("dve", lambda e: e.tensor_tensor(out=t2[:, :], in0=t1[:, 0, :], in1=lnr[:, 0, :], op=ALU.mult),
                                         reads=[r_t1[0], r_lnr[0]], writes=[r_t2])
                                  mix_out(j, t2[:, :], r_t2, eng="dve")

                              pending.append(epiC)
                      flush_pending()
                      chk(5 + 10 * L)
                      out_proj(src, src_res, dst, dst_res, qs, 0, dchan, is_out)


        except _Stop:
            pass

        P.run(nc, stack)
    return nc


def _prep_weights(w_in0, w_out0, w_in1, w_out1):
    qa = w_in0[:, 0:512]
    ka = w_in0[:, 512:640]
    va = w_in0[:, 640:768]
    qb = w_in0[:, 768:1280]
    kb = w_in0[:, 1280:1792]
    vb = w_in0[:, 1792:2304]
    gate = w_in0[:, 2304:3328]
    perm = []
    for j in range(4):
        perm += list(range(64 * j, 64 * j + 64)) + list(range(64 * (4 + j), 64 * (4 + j) + 64))
    perm = np.array(perm)
    mixperm = np.concatenate([perm, np.arange(512, 1024)])
    w0 = np.concatenate([ka, kb, vb, va, qa[:, perm], qb, gate[:, mixperm]], axis=1)
    wo0 = w_out0[mixperm, :]
    w1 = np.concatenate([w_in1[:, 1024:2048], w_in1[:, 2048:3072], w_in1[:, 0:1024], w_in1[:, 3072:4096]], axis=1)
    return (np.ascontiguousarray(w0, np.float32), np.ascontiguousarray(wo0, np.float32),
            np.ascontiguousarray(w1, np.float32), np.ascontiguousarray(w_out1, np.float32))


def _common_inputs(inp):
    w0, wo0, w1, wo1 = _prep_weights(np.asarray(inp["w_in0"]), np.asarray(inp["w_out0"]),
                                     np.asarray(inp["w_in1"]), np.asarray(inp["w_out1"]))
    vec64 = np.stack([np.asarray(inp[k], np.float32) for k in
                      ("a_q_norm", "a_k_norm", "b_q_norm", "b_k_norm", "c_q_norm", "c_k_norm",
                       "lambda_q1", "lambda_k1", "lambda_q2", "lambda_k2")]).astype(np.float32)
    norms = np.stack([np.asarray(inp["norm0"], np.float32), np.asarray(inp["norm1"], np.float32)])
    return dict(w_in0=w0, w_out0=wo0, w_in1=w1, w_out1=wo1, vec64=np.ascontiguousarray(vec64),
                subln=np.ascontiguousarray(np.asarray(inp["b_subln"], np.float32)), norms=np.ascontiguousarray(norms),
                cmat=_const_mats(), maskd=_mask_table())


def run_cfg(cfg, inp, xp_list, xs_list, pos_list, n_cores):
    nc = build(cfg)
    common = _common_inputs(inp)
    in_maps = []
    for c in range(n_cores):
        m = dict(common)
        if cfg["NP"]:
            m["xp"] = np.ascontiguousarray(xp_list[c], np.float32)
            m["ropeP"] = _rope_tables(np.arange(cfg["SP"]))
        if cfg["sample"]:
            m["xs"] = np.ascontiguousarray(xs_list[c], np.float32)
            m["ropeS"] = _rope_tables(pos_list[c])
        in_maps.append(m)
    res = run_bass_kernel_spmd(nc, in_maps, core_ids=list(range(n_cores)))
    return res.results


def kernel(**inp):
    x_prompt = np.asarray(inp["x_prompt"], np.float32)
    x_sample = np.asarray(inp["x_sample"], np.float32)
    n = 8
    cfg = dict(NP=4, SP=2048, sample=True, SS=4096, NQ0=3072, NQ1=2048)
    xp_list, xs_list, pos_list = [], [], []
    for c in range(n):
        xp_list.append(x_prompt[4 * c:4 * c + 4])
        sq_, half = c // 2, c % 2
        if half == 0:
            xs_list.append(x_sample[sq_])
            pos_list.append(np.arange(4096))
        else:
            xs_list.append(x_sample[sq_][::-1])
            pos_list.append(np.arange(4095, -1, -1))
    res = run_cfg(cfg, inp, xp_list, xs_list, pos_list, n)
    y_prompt = np.concatenate([res[c]["yp"] for c in range(n)], axis=0).astype(np.float32)
    y_sample = np.zeros_like(x_sample, dtype=np.float32)
    for c in range(n):
        sq_, half = c // 2, c % 2
        ys = res[c]["ys"]
        if half == 0:
            y_sample[sq_, 0:2048] = ys
        else:
            y_sample[sq_, 2048:4096] = ys[::-1]
    return (y_prompt, y_sample)
```

```python
import math
from contextlib import ExitStack

import numpy as np
import ml_dtypes

import concourse.bass as bass
import concourse.mybir as mybir
from concourse.bass_utils import run_bass_kernel_spmd

F32 = mybir.dt.float32
BF16 = mybir.dt.bfloat16
AF = mybir.ActivationFunctionType
ALU = mybir.AluOpType

D = 1024
EPS = 1e-6
MASK_U0 = 1408
MASK_W = 2944


class Chan:
    def __init__(self, name, step):
        self.name = name
        self.step = step
        self.handle = None
        self.count = 0


class Op:
    __slots__ = ("eng", "fn", "waits", "chan", "pos", "ticket", "need_inc", "is_dma")


class Res:
    __slots__ = ("name", "w", "r", "excl")

    def __init__(self, name, excl=False):
        self.name = name
        self.w = None
        self.r = {}
        self.excl = excl


ENGS = ("sp", "act", "dve", "pool", "pe")


class Prog:
    def __init__(self):
        self.ops = {e: [] for e in ENGS}
        self.echan = {e: Chan(e, 1) for e in ENGS}
        self.dchans = []
        self.synced = {e: {} for e in ENGS}
        self.out_ops = []

    def dchan(self, name):
        c = Chan(name, 16)
        self.dchans.append(c)
        return c

    def emit(self, eng, fn, reads=(), writes=(), dma=None):
        op = Op()
        op.eng = eng
        op.fn = fn
        op.waits = []
        op.need_inc = False
        op.ticket = None
        if dma is not None:
            op.chan = dma
            dma.count += 1
            op.pos = dma.count
            op.is_dma = True
        else:
            op.chan = self.echan[eng]
            op.pos = len(self.ops[eng])
            op.is_dma = False
        deps = {}

        own = self.echan[eng]

        def add(d, war=False):
            if d is None:
                return
            k = d.chan
            if k is own and not d.is_dma:
                if eng == "pe":
                    return
            if k not in deps or deps[k].pos < d.pos:
                deps[k] = d

        for r in reads:
            add(r.w)
            if r.excl:
                for k2, rr in r.r.items():
                    if k2 is not op.chan:
                        add(rr)
        for w in writes:
            add(w.w)
            for rr in w.r.values():
                add(rr, war=True)
        sy = self.synced[eng]
        for k, d in deps.items():
            if sy.get(k, -1) >= d.pos:
                continue
            op.waits.append(d)
            d.need_inc = True
            sy[k] = d.pos
        for r in reads:
            old = r.r.get(op.chan)
            if old is None or old.pos < op.pos:
                r.r[op.chan] = op
        for w in writes:
            w.w = op
            w.r = {}
        self.ops[eng].append(op)
        return op

    def finalize(self):
        for e in ENGS:
            t = 0
            for op in self.ops[e]:
                if op.is_dma:
                    op.ticket = op.pos * 16
                elif op.need_inc:
                    t += 1
                    op.ticket = t

    def run(self, nc, stack):
        self.finalize()
        for c in list(self.echan.values()) + self.dchans:
            c.handle = stack.enter_context(nc.semaphore("s_" + c.name))
        block = stack.enter_context(nc.Block())
        decos = {"sp": block.sync, "act": block.scalar, "dve": block.vector,
                 "pool": block.gpsimd, "pe": block.tensor}
        for eng in ENGS:
            ops = self.ops[eng]
            outs = self.out_ops if eng == "sp" else None

            def body(e, ops=ops, outs=outs):
                for op in ops:
                    for d in op.waits:
                        e.wait_ge(d.chan.handle, d.ticket)
                    ins = op.fn(e)
                    if op.is_dma:
                        ins.then_inc(op.chan.handle, 16)
                    elif op.need_inc:
                        ins.then_inc(op.chan.handle, 1)
                if outs is not None:
                    for ch in self.dchans:
                        if ch.count:
                            e.wait_ge(ch.handle, ch.count * 16)

            decos[eng](body)


def _rope_tables(pos):
    pos = np.asarray(pos, dtype=np.float64)
    S = pos.shape[0]
    out = np.zeros((4, 128, S), np.float32)
    f32 = (10000.0 ** (-(np.arange(0, 32, 2, dtype=np.float32) / np.float32(32)))).astype(np.float32)
    f16 = (500000.0 ** (-(np.arange(0, 16, 2, dtype=np.float32) / np.float32(16)))).astype(np.float32)
    row = np.floor(pos / 64.0).astype(np.float32)
    col = (pos - 64.0 * np.floor(pos / 64.0)).astype(np.float32)
    posf = pos.astype(np.float32)
    for p in range(128):
        d = p % 64
        if d < 32:
            ang = row * f32[d % 16]
        else:
            ang = col * f32[(d - 32) % 16]
        ang = ang.astype(np.float32)
        out[0, p] = np.cos(ang)
        out[1, p] = np.sin(ang)
        if d < 16:
            a2 = (posf * f16[d % 8]).astype(np.float32)
            out[2, p] = np.cos(a2)
            out[3, p] = np.sin(a2)
        else:
            out[2, p] = 1.0
            out[3, p] = 0.0
    return out


def _const_mats():
    cm = np.zeros((6, 128, 128), np.float32)
    cm[0] = np.eye(128)
    for b in range(2):
        cm[1, b * 64:(b + 1) * 64, b * 64:(b + 1) * 64] = 1.0 / 64.0
    cm[2] = 1.0 / 128.0
    cm[3] = 1.0
    for b in range(2):
        o = b * 64
        for blk in (0, 32):
            for i in range(16):
                cm[4, o + blk + i + 16, o + blk + i] = -1.0
                cm[4, o + blk + i, o + blk + i + 16] = 1.0
        for i in range(8):
            cm[5, o + i + 8, o + i] = -1.0
            cm[5, o + i, o + i + 8] = 1.0
    return np.ascontiguousarray(cm.transpose(1, 0, 2)).astype(ml_dtypes.bfloat16)


def _mask_table():
    kk = np.arange(128)[:, None]
    u = np.arange(MASK_W)[None, :]
    dl = u - kk - MASK_U0
    a = np.abs(dl)
    c = (a <= 64).astype(np.float32) + ((dl % 4 == 0) & (a <= 256)) + ((dl % 16 == 0) & (a <= 1024))
    return c.astype(ml_dtypes.bfloat16)


def build(cfg):
    NP = cfg["NP"]
    SP = cfg["SP"]
    has_s = cfg["sample"]
    SS, NQ0, NQ1 = cfg.get("SS", 4096), cfg.get("NQ0", 3072), cfg.get("NQ1", 2048)

    nc = bass.Bass("TRN2", target_bir_lowering=False)

    def dram(name, shape, dt, kind):
        return nc.dram_tensor(name, list(shape), dt, kind=kind).ap()

    EI, EO, IN = "ExternalInput", "ExternalOutput", "Internal"
    jobs = []
    if NP:
        xp = dram("xp", [NP, SP, D], F32, EI)
        yp = dram("yp", [NP, SP, D], F32, EO)
        ropeP = dram("ropeP", [4, 128, SP], F32, EI)
        x1p = dram("x1p", [NP, SP, D], F32, IN)
        for i in range(NP):
            jobs.append(dict(name=f"p{i}", x=xp[i], y=yp[i], x1=x1p[i], rope=ropeP, S=SP, nq0=SP, nq1=SP))
    if has_s:
        xs = dram("xs", [SS, D], F32, EI)
        ys = dram("ys", [NQ1, D], F32, EO)
        ropeS = dram("ropeS", [4, 128, SS], F32, EI)
        x1s = dram("x1s", [NQ0, D], F32, IN)
        jobs.append(dict(name="s", x=xs, y=ys, x1=x1s, rope=ropeS, S=SS, nq0=NQ0, nq1=NQ1))
    w_in0 = dram("w_in0", [D, 3328], F32, EI)
    w_out0 = dram("w_out0", [D, D], F32, EI)
    w_in1 = dram("w_in1", [D, 4096], F32, EI)
    w_out1 = dram("w_out1", [D, D], F32, EI)
    vec64 = dram("vec64", [10, 64], F32, EI)
    subln = dram("subln", [128], F32, EI)
    norms = dram("norms", [2, D], F32, EI)
    cmat = dram("cmat", [128, 6, 128], BF16, EI)
    maskd = dram("maskd", [128, MASK_W], BF16, EI)
    wsc0 = dram("wsc0", [128, 8, 3328], BF16, IN)
    wso0 = dram("wso0", [128, 8, D], BF16, IN)
    wsc1 = dram("wsc1", [128, 8, 4096], BF16, IN)
    wso1 = dram("wso1", [128, 8, D], BF16, IN)

    KVC = 0
    for j in jobs:
        S = j["S"]
        l0 = 5 * S + (S // 128) * 256 + (S // 128) * 512
        l1 = 8 * j["nq0"] + (j["nq0"] // 128) * 1024
        KVC = max(KVC, l0, l1)
    KVC = max(KVC, 2 * 8192 + 2 * 4096, cfg.get('KVC_MIN', 0))

    P = Prog()
    stack = ExitStack()
    with stack:
        def sb(name, shape, dt):
            return stack.enter_context(nc.sbuf_tensor(name, list(shape), dt))

        KV = sb("KV", [128, KVC], BF16)
        hT = sb("hT", [128, 8, 512], BF16)
        QT = sb("QT", [128, 8, 512], BF16)
        GT = sb("GT", [128, 8, 512], BF16)
        xin = sb("xin", [128, 2, D], F32)
        xnb = sb("xnb", [128, 2, D], BF16)
        W = [sb("W0", [128, 8, D], BF16), sb("W1", [128, 8, D], BF16)]
        rope = sb("rope", [128, 4, 512], F32)
        PT = sb("PT", [128, 6, 512], BF16)
        PM = sb("PM", [128, 6, 512], BF16)
        mask = sb("mask", [128, MASK_W], BF16)
        sq = sb("sq", [128, 2, 512], BF16)
        zb = sb("zb", [128, 2, 512], BF16)
        lnr = sb("lnr", [128, 2, 512], F32)
        t1 = sb("t1", [128, 2, 512], F32)
        t2 = sb("t2", [128, 512], F32)
        cm = sb("cm", [128, 6, 128], BF16)
        G = sb("G", [128, 12], F32)
        normT = sb("normT", [128, 16], F32)
        ssx = sb("ssx", [128, 8], F32)
        lam4 = sb("lam4", [128, 4, 64], F32)
        psb = [stack.enter_context(nc.psum_tensor(f"ps{i}", [128, 512], F32)) for i in range(8)]
        TPb = psb[6][:, :].bitcast(BF16)

        ident = cm[:, 0, :]
        bones64 = cm[:, 1, :]
        bones128 = cm[:, 2, :]
        ones = cm[:, 3, :]
        RA = cm[:, 4, :]
        RP = cm[:, 5, :]

        r_KV = {}

        def rkv(key):
            if key not in r_KV:
                r_KV[key] = Res("kv%s" % (key,))
            return r_KV[key]

        r_hm = [[Res(f"hm{c}_{s}") for s in range(4)] for c in range(8)]
        r_QT = [Res(f"qt{j}") for j in range(8)]
        r_GT = [Res(f"gt{j}") for j in range(8)]
        r_xin = [Res("xin0"), Res("xin1")]
        r_xnb = [Res("xnb0"), Res("xnb1")]
        r_W = [Res("W0"), Res("W1")]
        r_rope = Res("rope")
        r_PT = [Res(f"pt{i}") for i in range(6)]
        r_PM = [Res(f"pm{i}") for i in range(6)]
        r_mask = Res("mask")
        r_sq = [Res("sq0"), Res("sq1")]
        r_zb = [Res("zb0"), Res("zb1")]
        r_lnr = [Res("lnr0"), Res("lnr1")]
        r_t1 = [Res("t10"), Res("t11")]
        r_t2 = Res("t2")
        r_cm = Res("cm")
        r_G = Res("G")
        r_normT = Res("normT")
        r_ssx = Res("ssx")
        r_lam4 = Res("lam4")
        r_ps = [Res(f"ps{i}", excl=True) for i in range(8)]
        r_wsc = {k: [Res(k + "a"), Res(k + "b")] for k in ("wsc0", "wso0", "wsc1", "wso1")}
        r_kvstage = Res("kvstage")

        ch_xin = [P.dchan("dxin0"), P.dchan("dxin1")]
        ch_W = [P.dchan("dW0"), P.dchan("dW1")]
        ch_rope = P.dchan("drope")
        ch_const = P.dchan("dconst")
        ch_st = [P.dchan("dstore0"), P.dchan("dstore1")]
        ch_out = [P.dchan("dout0"), P.dchan("dout1")]
        ch_wst = [P.dchan("dwst0"), P.dchan("dwst1")]
        ch_wsto = [P.dchan("dwsto0"), P.dchan("dwsto1")]

        P.emit("sp", lambda e: e.dma_start(out=cm[:, :, :], in_=cmat), writes=[r_cm], dma=ch_const)
        P.emit("sp", lambda e: e.dma_start(out=mask[:, :], in_=maskd), writes=[r_mask], dma=ch_const)
        for i in range(6):
            for hb in range(2):
                P.emit("sp", lambda e, i=i, hb=hb: e.dma_start(
                    out=G[hb * 64:(hb + 1) * 64, i:i + 1], in_=vec64[i].rearrange("(p o) -> p o", o=1)),
                    writes=[r_G], dma=ch_const)
        P.emit("sp", lambda e: e.dma_start(out=G[:, 6:7], in_=subln.rearrange("(p o) -> p o", o=1)),
               writes=[r_G], dma=ch_const)
        for l in range(2):
            for c in range(8):
                P.emit("sp", lambda e, l=l, c=c: e.dma_start(
                    out=normT[:, l * 8 + c:l * 8 + c + 1],
                    in_=norms[l, c * 128:(c + 1) * 128].rearrange("(p o) -> p o", o=1)),
                    writes=[r_normT], dma=ch_const)
        for i in range(4):
            P.emit("sp", lambda e, i=i: e.dma_start(out=lam4[:, i, :], in_=vec64[6 + i].partition_broadcast(128)),
                   writes=[r_lam4], dma=ch_const)
        last_const = P.ops["sp"][-1]
        for rr in (r_cm, r_mask, r_G, r_normT, r_lam4):
            rr.w = last_const
        P.emit("dve", lambda e: e.scalar_tensor_tensor(out=t2[:, 0:64], in0=lam4[:, 0, :], scalar=1.0, in1=lam4[:, 1, :],
                                                       op0=ALU.mult, op1=ALU.mult, accum_out=ssx[:, 4:5]),
               reads=[r_lam4], writes=[r_t2, r_ssx])
        P.emit("dve", lambda e: e.scalar_tensor_tensor(out=t2[:, 64:128], in0=lam4[:, 2, :], scalar=1.0, in1=lam4[:, 3, :],
                                                       op0=ALU.mult, op1=ALU.mult, accum_out=ssx[:, 5:6]),
               reads=[r_lam4], writes=[r_t2, r_ssx])
        P.emit("act", lambda e: e.activation(out=ssx[:, 6:8], in_=ssx[:, 4:6], func=AF.Exp), reads=[r_ssx], writes=[r_ssx])
        P.emit("dve", lambda e: e.tensor_tensor(out=ssx[:, 4:5], in0=ssx[:, 7:8], in1=ssx[:, 6:7], op=ALU.subtract),
               reads=[r_ssx], writes=[r_ssx])
        P.emit("dve", lambda e: e.tensor_scalar(out=G[:, 7:8], in0=ssx[:, 4:5], scalar1=-0.2, scalar2=None, op0=ALU.add),
               reads=[r_ssx, r_G], writes=[r_G])
        P.emit("dve", lambda e: e.tensor_scalar(out=G[:, 6:7], in0=G[:, 6:7], scalar1=0.8, scalar2=None, op0=ALU.mult),
               reads=[r_G], writes=[r_G])
        P.emit("dve", lambda e: e.memset(G[:, 8:9], EPS), reads=[], writes=[r_G])

        stage_f = [KV[:, 0:8192].bitcast(F32), KV[:, 8192:16384].bitcast(F32)]
        stage_b = [KV[:, 16384:20480], KV[:, 20480:24576]]
        r_sf = [Res("sf0"), Res("sf1")]
        r_sbb = [Res("sb0"), Res("sb1")]
        wi = 0
        for (src, dst, rdst, ncols, nl) in ((w_in0, wsc0, "wsc0", 3328, 0), (w_out0, wso0, "wso0", D, None),
                                            (w_in1, wsc1, "wsc1", 4096, 1), (w_out1, wso1, "wso1", D, None)):
            for c in range(8):
                s = wi % 2
                wi += 1
                P.emit("sp", lambda e, s=s, c=c, src=src, ncols=ncols: e.dma_start(
                    out=stage_f[s][:, 0:ncols], in_=src[c * 128:(c + 1) * 128, :]),
                    writes=[r_sf[s]], dma=ch_wst[s])
                eng = "dve" if s == 0 else "pool"
                if nl is None:
                    P.emit(eng, lambda e, s=s, ncols=ncols: e.tensor_copy(out=stage_b[s][:, 0:ncols], in_=stage_f[s][:, 0:ncols]),
                           reads=[r_sf[s]], writes=[r_sbb[s]])
                else:
                    P.emit(eng, lambda e, s=s, ncols=ncols, col=nl * 8 + c: e.tensor_scalar(
                        out=stage_b[s][:, 0:ncols], in0=stage_f[s][:, 0:ncols], scalar1=normT[:, col:col + 1],
                        scalar2=None, op0=ALU.mult),
                        reads=[r_sf[s], r_normT], writes=[r_sbb[s]])
                P.emit("sp", lambda e, s=s, c=c, dst=dst, ncols=ncols: e.dma_start(out=dst[:, c, :], in_=stage_b[s][:, 0:ncols]),
                       reads=[r_sbb[s]], writes=[r_wsc[rdst][s]], dma=ch_wsto[s])
        kv_all_boundary = [r_sf[0], r_sf[1], r_sbb[0], r_sbb[1]]

        counters = dict(xs=0, qk=0, pt=0, sb=0)
        fscr = sb("fscr", [128, 4], F32)

        def fence(res_list):
            for k, eng in enumerate(("pool", "dve")):
                P.emit(eng, lambda e, k=k: e.memset(fscr[:, k:k + 1], 0.0), reads=[], writes=list(res_list))

        def load_w(slot, wname, wd, col0, n):
            for h in range(2):
                P.emit("sp", lambda e, h=h: e.dma_start(out=W[slot][:, h * 4:(h + 1) * 4, 0:n],
                                                        in_=wd[:, h * 4:(h + 1) * 4, col0:col0 + n]),
                       reads=r_wsc[wname], writes=[r_W[slot]], dma=ch_W[slot])

        cur = dict(h=hT, hres=r_hm)

        def norm_tile(src, src_res, tok0):
            hdst, hres = cur["h"], cur["hres"]
            for s in range(4):
                k = counters["xs"]
                counters["xs"] += 1
                sl = k % 2
                t0 = tok0 + s * 128
                P.emit("sp", lambda e, sl=sl, t0=t0: e.dma_start(out=xin[:, sl, :], in_=src[t0:t0 + 128, :]),
                       reads=src_res, writes=[r_xin[sl]], dma=ch_xin[sl])
                P.emit("act", lambda e, sl=sl: e.activation(out=xnb[:, sl, :], in_=xin[:, sl, :], func=AF.Square,
                                                            accum_out=ssx[:, sl:sl + 1]),
                       reads=[r_xin[sl]], writes=[r_xnb[sl], r_ssx])
                P.emit("act", lambda e, sl=sl: e.activation(out=ssx[:, 2 + sl:3 + sl], in_=ssx[:, sl:sl + 1], func=AF.Ln,
                                                            scale=1.0 / D, bias=G[:, 8:9]),
                       reads=[r_ssx, r_G], writes=[r_ssx])
                P.emit("act", lambda e, sl=sl: e.activation(out=ssx[:, 2 + sl:3 + sl], in_=ssx[:, 2 + sl:3 + sl], func=AF.Exp,
                                                            scale=-0.5),
                       reads=[r_ssx], writes=[r_ssx])
                P.emit("dve", lambda e, sl=sl: e.tensor_scalar(out=xnb[:, sl, :], in0=xin[:, sl, :],
                                                               scalar1=ssx[:, 2 + sl:3 + sl], scalar2=None, op0=ALU.mult),
                       reads=[r_xin[sl], r_ssx], writes=[r_xnb[sl]])
                for c in range(8):
                    P.emit("pe", lambda e, sl=sl, c=c: e.transpose(out=TPb[:, c * 128:(c + 1) * 128],
                                                                   in_=xnb[:, sl, c * 128:(c + 1) * 128], identity=ident),
                           reads=[r_xnb[sl], r_cm], writes=[r_ps[6]])
                P.emit("dve", lambda e, s=s: e.tensor_copy(out=hdst[:, :, s * 128:(s + 1) * 128],
                                                           in_=TPb.rearrange("p (c t) -> p c t", c=8)),
                       reads=[r_ps[6]], writes=[hres[c][s] for c in range(8)])

        def load_rope(job, tok0):
            P.emit("sp", lambda e: e.dma_start(out=rope[:, :, :],
                                               in_=job["rope"][:, :, tok0:tok0 + 512].rearrange("t p s -> p t s")),
                   writes=[r_rope], dma=ch_rope)

        qk_pending = []

        def flush_qk():
            while qk_pending:
                qk_pending.pop(0)()

        def proj_fm(slot, col0, bank):
            hsrc, hres = cur["h"], cur["hres"]
            for c in range(8):
                if c == 6:
                    flush_qk()
                P.emit("pe", lambda e, c=c: e.matmul(psb[bank][:, :], lhsT=W[slot][:, c, col0:col0 + 128], rhs=hsrc[:, c, :],
                                                     start=(c == 0), stop=(c == 7)),
                       reads=[r_W[slot]] + hres[c], writes=[r_ps[bank]])

        def qk_tile(slot, col0, dest, dest_res, gcol, rtype):
            i = counters["qk"]
            counters["qk"] += 1
            b = i % 2
            zbk, ssbk, rotbk = b, 2 + b, 4 + b
            ci, si, Rm = (0, 1, RA) if rtype == "A" else (2, 3, RP)
            proj_fm(slot, col0, zbk)
            P.emit("act", lambda e: e.activation(out=sq[:, b, :], in_=psb[zbk][:, :], func=AF.Square),
                   reads=[r_ps[zbk]], writes=[r_sq[b]])
            P.emit("act", lambda e: e.activation(out=zb[:, b, :], in_=psb[zbk][:, :], func=AF.Identity,
                                                 scale=G[:, gcol:gcol + 1]),
                   reads=[r_ps[zbk], r_G], writes=[r_zb[b]])
            def tail():
                P.emit("pe", lambda e: e.matmul(psb[ssbk][:, :], lhsT=bones64, rhs=sq[:, b, :], start=True, stop=True),
                       reads=[r_sq[b], r_cm], writes=[r_ps[ssbk]])
                P.emit("pe", lambda e: e.matmul(psb[rotbk][:, :], lhsT=Rm, rhs=zb[:, b, :], start=True, stop=True),
                       reads=[r_zb[b], r_cm], writes=[r_ps[rotbk]])
                P.emit("act", lambda e: e.activation(out=lnr[:, b, :], in_=psb[ssbk][:, :], func=AF.Ln, bias=G[:, 8:9]),
                       reads=[r_ps[ssbk], r_G], writes=[r_lnr[b]])
                P.emit("act", lambda e: e.activation(out=lnr[:, b, :], in_=lnr[:, b, :], func=AF.Exp, scale=-0.5),
                       reads=[r_lnr[b]], writes=[r_lnr[b]])
                P.emit("dve", lambda e: e.scalar_tensor_tensor(out=t1[:, b, :], in0=rope[:, ci, :], scalar=G[:, gcol:gcol + 1],
                                                               in1=psb[zbk][:, :], op0=ALU.mult, op1=ALU.mult),
                       reads=[r_ps[zbk], r_G, r_rope], writes=[r_t1[b]])
                P.emit("dve", lambda e: e.tensor_tensor(out=t2[:, :], in0=psb[rotbk][:, :], in1=rope[:, si, :], op=ALU.mult),
                       reads=[r_ps[rotbk], r_rope], writes=[r_t2])
                P.emit("dve", lambda e: e.tensor_tensor(out=t1[:, b, :], in0=t1[:, b, :], in1=t2[:, :], op=ALU.add),
                       reads=[r_t1[b], r_t2], writes=[r_t1[b]])
                P.emit("pool", lambda e: e.tensor_tensor(out=dest, in0=t1[:, b, :], in1=lnr[:, b, :], op=ALU.mult),
                       reads=[r_t1[b], r_lnr[b]], writes=dest_res)

            qk_pending.append(tail)

        def gate_tile(slot, col0, j):
            i = counters["qk"]
            counters["qk"] += 1
            b = i % 2
            proj_fm(slot, col0, b)
            P.emit("act", lambda e: e.activation(out=GT[:, j, :], in_=psb[b][:, :], func=AF.Silu),
                   reads=[r_ps[b]], writes=[r_GT[j]])

        def v_tiles(slot, ncols, evac):
            flush_qk()
            hsrc, hres = cur["h"], cur["hres"]
            for s in range(4):
                nb = (ncols + 511) // 512
                for n in range(nb):
                    wd = min(512, ncols - n * 512)
                    for c in range(8):
                        P.emit("pe", lambda e, c=c, n=n, wd=wd, s=s: e.matmul(
                            psb[n][:, 0:wd], lhsT=hsrc[:, c, s * 128:(s + 1) * 128], rhs=W[slot][:, c, n * 512:n * 512 + wd],
                            start=(c == 0), stop=(c == 7)),
                            reads=[r_W[slot], hres[c][s]], writes=[r_ps[n]])
                    evac(s, n, n, wd)

        def out_proj(src, src_res, dst, dst_res, tok0, slot, dchan, is_out):
            sls = []

            def ld(s):
                k = counters["xs"]
                counters["xs"] += 1
                sl = k % 2
                sls.append(sl)
                t0 = tok0 + s * 128
                P.emit("sp", lambda e: e.dma_start(out=xin[:, sl, :], in_=src[t0:t0 + 128, :]),
                       reads=src_res, writes=[r_xin[sl]], dma=ch_xin[sl])

            ld(0)
            ld(1)
            for s in range(4):
                sl = sls[s]
                t0 = tok0 + s * 128
                for n in range(2):
                    bank = 4 + n
                    for c in range(8):
                        P.emit("pe", lambda e, c=c, n=n, s=s, bank=bank: e.matmul(
                            psb[bank][:, :], lhsT=hT[:, c, s * 128:(s + 1) * 128], rhs=W[slot][:, c, n * 512:(n + 1) * 512],
                            start=(c == 0), stop=(c == 7)),
                            reads=[r_W[slot], r_hm[c][s]], writes=[r_ps[bank]])
                    P.emit("dve", lambda e, n=n, sl=sl, bank=bank: e.tensor_tensor(
                        out=xin[:, sl, n * 512:(n + 1) * 512], in0=psb[bank][:, :], in1=xin[:, sl, n * 512:(n + 1) * 512],
                        op=ALU.add),
                        reads=[r_ps[bank], r_xin[sl]], writes=[r_xin[sl]])
                op = P.emit("sp", lambda e, sl=sl, t0=t0: e.dma_start(out=dst[t0:t0 + 128, :], in_=xin[:, sl, :]),
                            reads=[r_xin[sl]], writes=[dst_res[sl]], dma=dchan[sl])
                if is_out:
                    P.out_ops.append(op)
                if s + 2 < 4:
                    ld(s + 2)

        pending = []

        def flush_pending():
            while pending:
                pending.pop(0)()

        def attn_pairs(kts, s_mm, pv_mm, masked, qs, sbanks=(0, 1, 2, 3)):
            kts = list(kts)
            n = len(kts)
            nr = len(sbanks)
            depth = nr // 2
            slots = {}
            cnt = [0]

            def issue_s(ii):
                kt = kts[ii]
                for m in range(2):
                    idx = cnt[0] % nr
                    cnt[0] += 1
                    r = sbanks[idx]
                    a = idx
                    fn, reads = s_mm(kt, m, r)
                    P.emit("pe", fn, reads=reads, writes=[r_ps[r]])
                    P.emit("act", lambda e, a=a, r=r: e.activation(out=PT[:, a, :], in_=psb[r][:, :], func=AF.Exp, scale=0.125),
                           reads=[r_ps[r]], writes=[r_PT[a]])
                    if masked:
                        off = qs - 128 * kt + MASK_U0
                        P.emit("dve", lambda e, a=a, off=off: e.tensor_tensor(out=PM[:, a, :], in0=PT[:, a, :],
                                                                              in1=mask[:, off:off + 512], op=ALU.mult),
                               reads=[r_PT[a], r_mask], writes=[r_PM[a]])
                        slots[(ii, m)] = (PM[:, a, :], r_PM[a])
                    else:
                        slots[(ii, m)] = (PT[:, a, :], r_PT[a])

            def issue_pv(ii):
                kt = kts[ii]
                for m in range(2):
                    src, rsrc = slots.pop((ii, m))
                    for fn, reads, writes in pv_mm(kt, m, src, ii == 0, ii == n - 1):
                        P.emit("pe", fn, reads=reads + [rsrc], writes=writes)

            for ii in range(min(depth - 1, n)):
                issue_s(ii)
            for ii in range(n):
                if ii + depth - 1 < n:
                    issue_s(ii + depth - 1)
                issue_pv(ii)
                if ii == min(5, n - 1):
                    flush_pending()

        def mix_out(j, pre, pre_res, eng="pool"):
            P.emit(eng, lambda e: e.tensor_tensor(out=hT[:, j, :], in0=pre, in1=GT[:, j, :], op=ALU.mult),
                   reads=[pre_res, r_GT[j]], writes=r_hm[j])

        kv_prev = [r_sf[0], r_sf[1], r_sbb[0], r_sbb[1]]
        STOP = cfg.get("stop", 99)

        class _Stop(Exception):
            pass

        def chk(level):
            if STOP <= level:
                raise _Stop()

        try:
          chk(0)
          for job in jobs:
              S = job["S"]
              nq0, nq1 = job["nq0"], job["nq1"]
              tag = job["name"]
              r_x1 = [Res("x1a" + tag), Res("x1b" + tag)]
              r_y = [Res("ya" + tag), Res("yb" + tag)]
              for L in range(2):
                  if L == 0:
                      n_kv, n_q = S, nq0
                      src, src_res, dst, dst_res, dchan, is_out = job["x"], [], job["x1"], r_x1, ch_st, False
                      wsc, wname, wso, woname = wsc0, "wsc0", wso0, "wso0"
                      kc0, kn, vc0, vn, qc0, gc0 = 0, 640, 640, 640, 1280, 2304
                  else:
                      n_kv, n_q = nq0, nq1
                      src, src_res, dst, dst_res, dchan, is_out = job["x1"], r_x1, job["y"], r_y, ch_out, True
                      wsc, wname, wso, woname = wsc1, "wsc1", wso1, "wso1"
                      kc0, kn, vc0, vn, qc0, gc0 = 0, 1024, 1024, 1024, 2048, 3072
                  nkt = n_kv // 128
                  ntt = n_kv // 512
                  KA = 0
                  KB = [n_kv + h * n_kv for h in range(4)]
                  VA = 5 * n_kv
                  VB = VA + nkt * 256
                  KC = [jj * n_kv for jj in range(8)]
                  VC = 8 * n_kv
                  rk = [Res(f"k{tag}{L}_{tt}") for tt in range(ntt)]
                  rv = [Res(f"v{tag}{L}_{tt}") for tt in range(ntt)]
                  r_ones = Res(f"ones{tag}{L}")

                  load_w(0, wname, wsc, kc0, kn)
                  load_w(1, wname, wsc, vc0, vn)
                  chk(1 + 10 * L)
                  fence(kv_prev)
                  kv_prev = rk + rv + [r_ones]
                  kv_end = (VB + nkt * 512) if L == 0 else (VC + nkt * 1024)
                  use_hall = (kv_end + 8 * n_kv) <= KVC and not cfg.get("no_hall", False)
                  if use_hall:
                      hall3 = KV[:, kv_end:kv_end + 8 * n_kv].rearrange("p (c t) -> p c t", c=8)
                      r_hall = [[[Res(f"hall{tag}{L}_{tt}_{c}_{s_}") for s_ in range(4)] for c in range(8)] for tt in range(ntt)]
                      for tt_ in range(ntt):
                          for c_ in range(8):
                              kv_prev = kv_prev + r_hall[tt_][c_]
                  if L == 0:
                      P.emit("pool", lambda e, VA=VA, nkt=nkt: e.memset(
                          KV[:, VA:VA + nkt * 256].rearrange("p (k c) -> p k c", c=256)[:, :, 64:192], 1.0),
                          reads=[], writes=[r_ones])
                  for tt in range(ntt):
                      chk(1.2 + 10 * L)
                      if use_hall:
                          cur["h"], cur["hres"] = hall3[:, :, tt * 512:(tt + 1) * 512], r_hall[tt]
                      else:
                          cur["h"], cur["hres"] = hT, r_hm
                      norm_tile(src, src_res, tt * 512)
                      chk(1.4 + 10 * L)
                      load_rope(job, tt * 512)
                      if L == 0:
                          qk_tile(0, 0, KV[:, KA + tt * 512:KA + (tt + 1) * 512], [rk[tt]], 1, "A")
                          chk(1.6)
                          for h in range(4):
                              qk_tile(0, 128 + h * 128, KV[:, KB[h] + tt * 512:KB[h] + (tt + 1) * 512], [rk[tt]], 3, "P")

                          def evac(s, n, bank, wd, tt=tt, VA=VA, VB=VB, rv=rv):
                              kt = tt * 4 + s
                              if n == 0:
                                  P.emit("dve", lambda e: e.tensor_copy(out=KV[:, VB + kt * 512:VB + (kt + 1) * 512],
                                                                        in_=psb[bank][:, :]),
                                         reads=[r_ps[bank]], writes=[rv[tt]])
                              else:
                                  for a in range(2):
                                      P.emit("dve", lambda e, a=a: e.tensor_copy(
                                          out=KV[:, VA + kt * 256 + a * 192:VA + kt * 256 + a * 192 + 64],
                                          in_=psb[bank][:, a * 64:(a + 1) * 64]),
                                          reads=[r_ps[bank]], writes=[rv[tt]])
                      else:
                          for jj in range(8):
                              qk_tile(0, jj * 128, KV[:, KC[jj] + tt * 512:KC[jj] + (tt + 1) * 512], [rk[tt]], 5, "P")

                          def evac(s, n, bank, wd, tt=tt, VC=VC, rv=rv):
                              kt = tt * 4 + s
                              P.emit("dve", lambda e: e.tensor_copy(
                                  out=KV[:, VC + kt * 1024 + n * 512:VC + kt * 1024 + (n + 1) * 512], in_=psb[bank][:, :]),
                                  reads=[r_ps[bank]], writes=[rv[tt]])
                      chk(1.8 + 10 * L)
                      v_tiles(1, vn, evac)

                  chk(2 + 10 * L)
                  load_w(1, wname, wsc, gc0, 1024)
                  load_w(0, wname, wsc, qc0, 1024)
                  for g in range(n_q // 512):
                      qs = g * 512
                      if use_hall:
                          cur["h"], cur["hres"] = hall3[:, :, qs:qs + 512], r_hall[g]
                      else:
                          cur["h"], cur["hres"] = hT, r_hm
                          norm_tile(src, src_res, qs)
                      load_rope(job, qs)
                      if g > 0:
                          load_w(0, wname, wsc, qc0, 1024)
                      for j in range(8):
                          gate_tile(1, j * 128, j)
                      if L == 0:
                          for j in range(4):
                              qk_tile(0, j * 128, QT[:, j, :], [r_QT[j]], 0, "A")
                          for j in range(4, 8):
                              qk_tile(0, j * 128, QT[:, j, :], [r_QT[j]], 2, "P")
                      else:
                          for j in range(8):
                              qk_tile(0, j * 128, QT[:, j, :], [r_QT[j]], 4, "P")
                      flush_qk()
                      load_w(0, woname, wso, 0, 1024)
                      if L == 0:
                          chk(3)
                          for j in range(4):
                              def s_mm(kt, m, r, j=j):
                                  rows = slice(m * 64, (m + 1) * 64)
                                  lhs = KV[rows, KA + kt * 128:KA + (kt + 1) * 128]
                                  rhs = QT[rows, j, :]
                                  return (lambda e: e.matmul(psb[r][:, :], lhsT=lhs, rhs=rhs, start=True, stop=True),
                                          [rk[kt // 4], r_QT[j]])

                              def pv_mm(kt, m, srcp, first, last):
                                  c0 = VA + kt * 256 + m * 128
                                  bank = 4 + m
                                  return [(lambda e: e.matmul(psb[bank][:, :], lhsT=KV[:, c0:c0 + 128], rhs=srcp,
                                                              start=first, stop=last),
                                           [rv[kt // 4], r_ones], [r_ps[bank]])]

                              attn_pairs(range(nkt), s_mm, pv_mm, False, 0, sbanks=(0, 1, 2, 3, 6, 7))
                              P.emit("dve", lambda e: e.tensor_copy(out=t1[:, 0, :], in_=psb[4][:, :]), reads=[r_ps[4]], writes=[r_t1[0]])
                              P.emit("dve", lambda e: e.tensor_copy(out=t1[:, 1, :], in_=psb[5][:, :]), reads=[r_ps[5]], writes=[r_t1[1]])

                              def epiA(j=j):
                                  P.emit("dve", lambda e: e.reciprocal(out=lnr[0:64, 0, :], in_=t1[64:128, 0, :]),
                                         reads=[r_t1[0]], writes=[r_lnr[0]])
                                  P.emit("dve", lambda e: e.reciprocal(out=lnr[64:128, 0, :], in_=t1[0:64, 1, :]),
                                         reads=[r_t1[1]], writes=[r_lnr[0]])
                                  P.emit("pool", lambda e: e.tensor_tensor(out=t2[0:64, :], in0=t1[0:64, 0, :], in1=lnr[0:64, 0, :], op=ALU.mult),
                                         reads=[r_t1[0], r_lnr[0]], writes=[r_t2])
                                  P.emit("pool", lambda e: e.tensor_tensor(out=t2[64:128, :], in0=t1[64:128, 1, :], in1=lnr[64:128, 0, :], op=ALU.mult),
                                         reads=[r_t1[1], r_lnr[0]], writes=[r_t2])
                                  mix_out(j, t2[:, :], r_t2)

                              pending.append(epiA)
                          chk(4)
                          for h in range(4):
                              j = 4 + h

                              def s_mm(kt, m, r, j=j, h=h):
                                  rows = slice(m * 64, (m + 1) * 64)
                                  lhs = KV[rows, KB[h] + kt * 128:KB[h] + (kt + 1) * 128]
                                  rhs = QT[rows, j, :]
                                  return (lambda e: e.matmul(psb[r][:, :], lhsT=lhs, rhs=rhs, start=True, stop=True),
                                          [rk[kt // 4], r_QT[j]])

                              def pv_mm(kt, m, srcp, first, last, h=h):
                                  c0 = VB + kt * 512 + h * 128
                                  bo, bd = 4 + 2 * m, 5 + 2 * m
                                  return [(lambda e: e.matmul(psb[bo][:, :], lhsT=KV[:, c0:c0 + 128], rhs=srcp,
                                                              start=first, stop=last),
                                           [rv[kt // 4]], [r_ps[bo]]),
                                          (lambda e: e.matmul(psb[bd][:, :], lhsT=ones, rhs=srcp, start=first, stop=last),
                                           [r_cm], [r_ps[bd]])]

                              attn_pairs(range(nkt), s_mm, pv_mm, False, 0)
                              P.emit("dve", lambda e: e.tensor_copy(out=t1[:, 0, :], in_=psb[4][:, :]), reads=[r_ps[4]], writes=[r_t1[0]])
                              P.emit("act", lambda e: e.activation(out=lnr[:, 0, :], in_=psb[5][:, :], func=AF.Ln),
                                     reads=[r_ps[5]], writes=[r_lnr[0]])
                              P.emit("dve", lambda e: e.tensor_copy(out=t1[:, 1, :], in_=psb[6][:, :]), reads=[r_ps[6]], writes=[r_t1[1]])
                              P.emit("act", lambda e: e.activation(out=lnr[:, 1, :], in_=psb[7][:, :], func=AF.Ln),
                                     reads=[r_ps[7]], writes=[r_lnr[1]])

                              def epiB(j=j):
                                  for m in range(2):
                                      P.emit("act", lambda e, m=m: e.activation(out=lnr[:, m, :], in_=lnr[:, m, :], func=AF.Exp, scale=-1.0),
                                             reads=[r_lnr[m]], writes=[r_lnr[m]])
                                      P.emit("pool", lambda e, m=m: e.tensor_tensor(out=t1[:, m, :], in0=t1[:, m, :], in1=lnr[:, m, :], op=ALU.mult),
                                             reads=[r_t1[m], r_lnr[m]], writes=[r_t1[m]])
                                  P.emit("dve", lambda e: e.scalar_tensor_tensor(out=t2[:, :], in0=t1[:, 1, :], scalar=G[:, 7:8],
                                                                                 in1=t1[:, 0, :], op0=ALU.mult, op1=ALU.add),
                                         reads=[r_t1[0], r_t1[1], r_G], writes=[r_t2])
                                  P.emit("act", lambda e: e.activation(out=sq[:, 0, :], in_=t2[:, :], func=AF.Square),
                                         reads=[r_t2], writes=[r_sq[0]])
                                  rb = 0
                                  counters["sb"] += 1
                                  P.emit("pe", lambda e: e.matmul(psb[rb][:, :], lhsT=bones128, rhs=sq[:, 0, :], start=True, stop=True),
                                         reads=[r_sq[0], r_cm], writes=[r_ps[rb]])
                                  P.emit("act", lambda e: e.activation(out=lnr[:, 0, :], in_=psb[rb][:, :], func=AF.Ln, bias=G[:, 8:9]),
                                         reads=[r_ps[rb], r_G], writes=[r_lnr[0]])
                                  P.emit("act", lambda e: e.activation(out=lnr[:, 0, :], in_=lnr[:, 0, :], func=AF.Exp, scale=-0.5),
                                         reads=[r_lnr[0]], writes=[r_lnr[0]])
                                  P.emit("dve", lambda e: e.scalar_tensor_tensor(out=t1[:, 0, :], in0=t2[:, :], scalar=G[:, 6:7],
                                                                                 in1=lnr[:, 0, :], op0=ALU.mult, op1=ALU.mult),
                                         reads=[r_t2, r_G, r_lnr[0]], writes=[r_t1[0]])
                                  mix_out(j, t1[:, 0, :], r_t1[0])

                              pending.append(epiB)
                      else:
                          chk(13)
                          kt_lo = max(0, (qs - 1024) // 128)
                          kt_hi = min(nkt, (qs + 512 + 1024) // 128)
                          for j in range(8):
                              def s_mm(kt, m, r, j=j):
                                  rows = slice(m * 64, (m + 1) * 64)
                                  lhs = KV[rows, KC[j] + kt * 128:KC[j] + (kt + 1) * 128]
                                  rhs = QT[rows, j, :]
                                  return (lambda e: e.matmul(psb[r][:, :], lhsT=lhs, rhs=rhs, start=True, stop=True),
                                          [rk[kt // 4], r_QT[j]])

                              def pv_mm(kt, m, srcp, first, last, j=j):
                                  c0 = VC + kt * 1024 + (2 * j + m) * 64
                                  rows = slice(m * 64, (m + 1) * 64)
                                  return [(lambda e: e.matmul(psb[4][rows, :], lhsT=KV[:, c0:c0 + 64], rhs=srcp,
                                                              start=first, stop=last),
                                           [rv[kt // 4]], [r_ps[4]]),
                                          (lambda e: e.matmul(psb[5][rows, :], lhsT=ones[:, 0:64], rhs=srcp,
                                                              start=first, stop=last),
                                           [r_cm], [r_ps[5]])]

                              attn_pairs(range(kt_lo, kt_hi), s_mm, pv_mm, True, qs, sbanks=(0, 1, 2, 3, 6, 7))
                              P.emit("dve", lambda e: e.tensor_copy(out=t1[:, 0, :], in_=psb[4][:, :]), reads=[r_ps[4]], writes=[r_t1[0]])
                              P.emit("act", lambda e: e.activation(out=lnr[:, 0, :], in_=psb[5][:, :], func=AF.Ln),
                                     reads=[r_ps[5]], writes=[r_lnr[0]])

                              def epiC(j=j):
                                  P.emit("act", lambda e: e.activation(out=lnr[:, 0, :], in_=lnr[:, 0, :], func=AF.Exp, scale=-1.0),
                                         reads=[r_lnr[0]], writes=[r_lnr[0]])
                                  P.emit("dve", lambda e: e.tensor_tensor(out=t2[:, :], in0=t1[:, 0, :], in1=lnr[:, 0, :], op=ALU.mult),
                                         reads=[r_t1[0], r_lnr[0]], writes=[r_t2])
                                  mix_out(j, t2[:, :], r_t2, eng="dve")

                              pending.append(epiC)
                      flush_pending()
                      chk(5 + 10 * L)
                      out_proj(src, src_res, dst, dst_res, qs, 0, dchan, is_out)


        except _Stop:
            pass

        P.run(nc, stack)
    return nc


def _prep_weights(w_in0, w_out0, w_in1, w_out1):
    qa = w_in0[:, 0:512]
    ka = w_in0[:, 512:640]
    va = w_in0[:, 640:768]
    qb = w_in0[:, 768:1280]
    kb = w_in0[:, 1280:1792]
    vb = w_in0[:, 1792:2304]
    gate = w_in0[:, 2304:3328]
    perm = []
    for j in range(4):
        perm += list(range(64 * j, 64 * j + 64)) + list(range(64 * (4 + j), 64 * (4 + j) + 64))
    perm = np.array(perm)
    mixperm = np.concatenate([perm, np.arange(512, 1024)])
    w0 = np.concatenate([ka, kb, vb, va, qa[:, perm], qb, gate[:, mixperm]], axis=1)
    wo0 = w_out0[mixperm, :]
    w1 = np.concatenate([w_in1[:, 1024:2048], w_in1[:, 2048:3072], w_in1[:, 0:1024], w_in1[:, 3072:4096]], axis=1)
    return (np.ascontiguousarray(w0, np.float32), np.ascontiguousarray(wo0, np.float32),
            np.ascontiguousarray(w1, np.float32), np.ascontiguousarray(w_out1, np.float32))


def _common_inputs(inp):
    w0, wo0, w1, wo1 = _prep_weights(np.asarray(inp["w_in0"]), np.asarray(inp["w_out0"]),
                                     np.asarray(inp["w_in1"]), np.asarray(inp["w_out1"]))
    vec64 = np.stack([np.asarray(inp[k], np.float32) for k in
                      ("a_q_norm", "a_k_norm", "b_q_norm", "b_k_norm", "c_q_norm", "c_k_norm",
                       "lambda_q1", "lambda_k1", "lambda_q2", "lambda_k2")]).astype(np.float32)
    norms = np.stack([np.asarray(inp["norm0"], np.float32), np.asarray(inp["norm1"], np.float32)])
    return dict(w_in0=w0, w_out0=wo0, w_in1=w1, w_out1=wo1, vec64=np.ascontiguousarray(vec64),
                subln=np.ascontiguousarray(np.asarray(inp["b_subln"], np.float32)), norms=np.ascontiguousarray(norms),
                cmat=_const_mats(), maskd=_mask_table())


def run_cfg(cfg, inp, xp_list, xs_list, pos_list, n_cores):
    nc = build(cfg)
    common = _common_inputs(inp)
    in_maps = []
    for c in range(n_cores):
        m = dict(common)
        if cfg["NP"]:
            m["xp"] = np.ascontiguousarray(xp_list[c], np.float32)
            m["ropeP"] = _rope_tables(np.arange(cfg["SP"]))
        if cfg["sample"]:
            m["xs"] = np.ascontiguousarray(xs_list[c], np.float32)
            m["ropeS"] = _rope_tables(pos_list[c])
        in_maps.append(m)
    res = run_bass_kernel_spmd(nc, in_maps, core_ids=list(range(n_cores)))
    return res.results


def kernel(**inp):
    x_prompt = np.asarray(inp["x_prompt"], np.float32)
    x_sample = np.asarray(inp["x_sample"], np.float32)
    n = 8
    cfg = dict(NP=4, SP=2048, sample=True, SS=4096, NQ0=3072, NQ1=2048)
    xp_list, xs_list, pos_list = [], [], []
    for c in range(n):
        xp_list.append(x_prompt[4 * c:4 * c + 4])
        sq_, half = c // 2, c % 2
        if half == 0:
            xs_list.append(x_sample[sq_])
            pos_list.append(np.arange(4096))
        else:
            xs_list.append(x_sample[sq_][::-1])
            pos_list.append(np.arange(4095, -1, -1))
    res = run_cfg(cfg, inp, xp_list, xs_list, pos_list, n)
    y_prompt = np.concatenate([res[c]["yp"] for c in range(n)], axis=0).astype(np.float32)
    y_sample = np.zeros_like(x_sample, dtype=np.float32)
    for c in range(n):
        sq_, half = c // 2, c % 2
        ys = res[c]["ys"]
        if half == 0:
            y_sample[sq_, 0:2048] = ys
        else:
            y_sample[sq_, 2048:4096] = ys[::-1]
    return (y_prompt, y_sample)
```
